# Optimizing a Trainium2 kernel written in Bass

```python
import math
import jax
import jax.numpy as jnp
from jax import lax
import numpy as np

D_MODEL = 1024
BATCH = 16
SEQ = 2048
DEPTH = 4

CTX_LEN = 256
GRID_W = 64

FOURIER_WIDTH = D_MODEL // 4
FOURIER_GROUPS = 4
FOURIER_GROUP_DIM = FOURIER_WIDTH // FOURIER_GROUPS
POOL_WINDOWS = (2, 4, 8, 16)
POOL_WIDTH = D_MODEL // 4
POOL_GROUP_DIM = POOL_WIDTH // len(POOL_WINDOWS)
HEAD_DIM = 64
Q_WIDTH = D_MODEL // 2
N_HEADS = Q_WIDTH // HEAD_DIM
N_KV_HEADS = N_HEADS // 4
GQA_GROUP = N_HEADS // N_KV_HEADS
KV_WIDTH = N_KV_HEADS * HEAD_DIM
ROPE_THETA = 10000.0
ROPE_AX_FREQS = HEAD_DIM // 4
Q_BLOCK = 128
N_BRANCHES = 3
OFF_P = FOURIER_WIDTH
OFF_Q = OFF_P + POOL_WIDTH
OFF_K = OFF_Q + Q_WIDTH
OFF_V = OFF_K + KV_WIDTH
OFF_G = OFF_V + KV_WIDTH
IN_COLS = OFF_G + N_BRANCHES * D_MODEL
PEER_HEADS = 8
PEER_KEYS = 128
PEER_EXPERTS = PEER_KEYS * PEER_KEYS
PEER_TOPK = 16
PEER_KEY_DIM = 128
PEER_QDIM = 2 * PEER_KEY_DIM
PEER_CHUNK = 128
EPS = 1e-6

kernel_name = 'hybrid_fourier_pool_gqa_peer_dit'


def rmsnorm(x, g):
    x32 = x.astype(jnp.float32)
    y = x32 * lax.rsqrt(jnp.mean(x32 * x32, axis=-1, keepdims=True) + EPS)
    return (y * g.astype(jnp.float32)).astype(x.dtype)


def modulate(h, shift, scale):
    return h * (1 + scale) + shift


def axial_rope_angles(L):
    rows = L // GRID_W
    row = jnp.repeat(jnp.arange(rows), GRID_W).astype(jnp.float32)
    col = jnp.tile(jnp.arange(GRID_W), rows).astype(jnp.float32)
    inv = ROPE_THETA ** (-jnp.arange(ROPE_AX_FREQS, dtype=jnp.float32) / ROPE_AX_FREQS)
    return row[:, None] * inv, col[:, None] * inv


def rope_rotate(x, ang):
    F = ang.shape[-1]
    cos = jnp.cos(ang)[None, :, None, :].astype(x.dtype)
    sin = jnp.sin(ang)[None, :, None, :].astype(x.dtype)
    x1, x2 = x[..., :F], x[..., F:]
    return jnp.concatenate([x1 * cos - x2 * sin, x2 * cos + x1 * sin], axis=-1)


def apply_axial_rope(x, ang_row, ang_col):
    half = HEAD_DIM // 2
    return jnp.concatenate([rope_rotate(x[..., :half], ang_row),
                            rope_rotate(x[..., half:], ang_col)], axis=-1)


def split_proj(P):
    B, L, _ = P.shape
    f = P[..., :OFF_P]
    p = P[..., OFF_P:OFF_Q]
    q = P[..., OFF_Q:OFF_K].reshape(B, L, N_HEADS, HEAD_DIM)
    k = P[..., OFF_K:OFF_V].reshape(B, L, N_KV_HEADS, HEAD_DIM)
    v = P[..., OFF_V:OFF_G].reshape(B, L, N_KV_HEADS, HEAD_DIM)
    gates = jax.nn.sigmoid(P[..., OFF_G:]).reshape(B, L, N_BRANCHES, D_MODEL)
    return f, p, q, k, v, gates


def fourier_branch(f, w):
    B, L, _ = f.shape
    fg = f.reshape(B, L, FOURIER_GROUPS, FOURIER_GROUP_DIM).astype(jnp.float32)
    re = jnp.fft.fftn(fg, axes=(1, 3), norm='ortho').real
    return re.reshape(B, L, FOURIER_WIDTH).astype(f.dtype) @ w


def pool_branch(p, w_grp, scale, w_proj):
    B, L, _ = p.shape
    n_g = len(POOL_WINDOWS)
    pg = p.reshape(B, L, n_g, POOL_GROUP_DIM)
    cs = jnp.concatenate([jnp.zeros((B, 1, n_g, POOL_GROUP_DIM), jnp.float32),
                          jnp.cumsum(pg.astype(jnp.float32), axis=1)], axis=1)
    t = jnp.arange(L)
    means = []
    for gi, win in enumerate(POOL_WINDOWS):
        lo = jnp.maximum(t - win // 2, 0)
        hi = jnp.minimum(t + win - win // 2, L)
        s = cs[:, hi, gi] - cs[:, lo, gi]
        means.append(s / (hi - lo).astype(jnp.float32)[None, :, None])
    pooled = jnp.stack(means, axis=2).astype(p.dtype) - pg
    mixed = jnp.einsum('blgc,gce->blge', pooled, w_grp).reshape(B, L, POOL_WIDTH) * scale
    return mixed @ w_proj


def attend(q, k, v):
    s = jnp.einsum('bqhgd,bkhd->bhgqk', q, k).astype(jnp.float32) * (HEAD_DIM ** -0.5)
    p = jax.nn.softmax(s, axis=-1).astype(v.dtype)
    return jnp.einsum('bhgqk,bkhd->bqhgd', p, v)


def latent_attention(q, k_all, v_all):
    B, L = q.shape[0], q.shape[1]
    nb = L // Q_BLOCK
    qb = q.reshape(B, nb, Q_BLOCK, N_KV_HEADS, GQA_GROUP, HEAD_DIM).swapaxes(0, 1)
    ob = lax.map(lambda qq: attend(qq, k_all, v_all), qb)
    return ob.swapaxes(0, 1).reshape(B, L, Q_WIDTH)


def merge(yf, yp, ya, gates, w_out):
    return (gates[:, :, 0] * yf + gates[:, :, 1] * yp + gates[:, :, 2] * ya) @ w_out


def peer(h, wq, sub_keys, u_tab, v_tab):
    B, L, D = h.shape
    xs = h.reshape(-1, PEER_CHUNK, D)

    def block(xc):
        C = xc.shape[0]
        q = (xc @ wq).reshape(C, PEER_HEADS, 2, PEER_KEY_DIM)
        s = jnp.einsum('chpd,hpnd->chpn', q, sub_keys).astype(jnp.float32)
        s1, i1 = lax.top_k(s[:, :, 0], PEER_TOPK)
        s2, i2 = lax.top_k(s[:, :, 1], PEER_TOPK)
        cand = (s1[..., :, None] + s2[..., None, :]).reshape(C, PEER_HEADS, PEER_TOPK * PEER_TOPK)
        cidx = (i1[..., :, None] * PEER_KEYS + i2[..., None, :]).reshape(C, PEER_HEADS, PEER_TOPK * PEER_TOPK)
        top, pos = lax.top_k(cand, PEER_TOPK)
        eidx = jnp.take_along_axis(cidx, pos, axis=-1)
        g = jax.nn.softmax(top, axis=-1).astype(xc.dtype)
        a = jax.nn.gelu(jnp.einsum('chkd,cd->chk', u_tab[eidx], xc))
        return jnp.einsum('chk,chkd->cd', g * a, v_tab[eidx])

    return lax.map(block, xs).reshape(B, L, D)


def setup_inputs(seed: int = 0) -> dict:
    key = jax.random.key(seed)
    ks = jax.random.split(key, 24)
    f32 = jnp.float32

    def nrm(k, shape, scale):
        return jax.random.normal(k, shape, f32) * scale

    def gain(k, shape):
        return 1.0 + 0.02 * jax.random.normal(k, shape, f32)

    return {
        'x': nrm(ks[0], (BATCH, SEQ, D_MODEL), 1.0),
        'c': nrm(ks[1], (BATCH, D_MODEL), 1.0),
        'ctx': nrm(ks[2], (BATCH, CTX_LEN, D_MODEL), 1.0),
        'c_ctx': nrm(ks[3], (D_MODEL,), 1.0),
        'ada_w': nrm(ks[4], (DEPTH, D_MODEL, 6 * D_MODEL), 0.5 * D_MODEL ** -0.5),
        'ada_b': nrm(ks[5], (DEPTH, 6 * D_MODEL), 0.02),
        'norm_mix': gain(ks[6], (DEPTH, D_MODEL)),
        'w_in': nrm(ks[7], (DEPTH, D_MODEL, IN_COLS), D_MODEL ** -0.5),
        'fourier_w': nrm(ks[8], (DEPTH, FOURIER_WIDTH, D_MODEL), FOURIER_WIDTH ** -0.5),
        'pool_w': nrm(ks[9], (DEPTH, len(POOL_WINDOWS), POOL_GROUP_DIM, POOL_GROUP_DIM), POOL_GROUP_DIM ** -0.5),
        'pool_scale': gain(ks[10], (DEPTH, POOL_WIDTH)),
        'pool_proj': nrm(ks[11], (DEPTH, POOL_WIDTH, D_MODEL), POOL_WIDTH ** -0.5),
        'q_norm': gain(ks[12], (DEPTH, HEAD_DIM)),
        'k_norm': gain(ks[13], (DEPTH, HEAD_DIM)),
        'attn_proj': nrm(ks[14], (DEPTH, Q_WIDTH, D_MODEL), Q_WIDTH ** -0.5),
        'w_out': nrm(ks[15], (DEPTH, D_MODEL, D_MODEL), D_MODEL ** -0.5),
        'norm_ffn': gain(ks[16], (DEPTH, D_MODEL)),
        'peer_wq': nrm(ks[17], (DEPTH, D_MODEL, PEER_HEADS * PEER_QDIM), D_MODEL ** -0.5),
        'peer_keys': nrm(ks[18], (DEPTH, PEER_HEADS, 2, PEER_KEYS, PEER_KEY_DIM), PEER_KEY_DIM ** -0.5),
        'peer_u': nrm(ks[19], (DEPTH, PEER_EXPERTS, D_MODEL), D_MODEL ** -0.5),
        'peer_v': nrm(ks[20], (DEPTH, PEER_EXPERTS, D_MODEL), PEER_HEADS ** -0.5),
    }


def reference(x, c, ctx, c_ctx, ada_w, ada_b, norm_mix, w_in, fourier_w, pool_w, pool_scale,
              pool_proj, q_norm, k_norm, attn_proj, w_out, norm_ffn, peer_wq, peer_keys, peer_u, peer_v):
    B, L, _ = x.shape
    ang_row, ang_col = axial_rope_angles(L)
    for l in range(DEPTH):
        last = l == DEPTH - 1
        m_lat = jnp.split((jax.nn.silu(c) @ ada_w[l] + ada_b[l])[:, None, :], 6, axis=-1)
        m_ctx = jnp.split(jax.nn.silu(c_ctx) @ ada_w[l] + ada_b[l], 6, axis=-1)

        h = modulate(rmsnorm(x, norm_mix[l]), m_lat[0], m_lat[1])
        hc = modulate(rmsnorm(ctx, norm_mix[l]), m_ctx[0], m_ctx[1])
        f, p, q, k, v, gates = split_proj(h @ w_in[l])
        fc, pc, qc, kc, vc, gates_c = split_proj(hc @ w_in[l])

        q = apply_axial_rope(rmsnorm(q, q_norm[l]), ang_row, ang_col)
        k = apply_axial_rope(rmsnorm(k, k_norm[l]), ang_row, ang_col)
        kc = rmsnorm(kc, k_norm[l])
        k_all = jnp.concatenate([k, kc], axis=1)
        v_all = jnp.concatenate([v, vc], axis=1)
        q = q.reshape(B, L, N_KV_HEADS, GQA_GROUP, HEAD_DIM)
        ya = latent_attention(q, k_all, v_all) @ attn_proj[l]
        yf = fourier_branch(f, fourier_w[l])
        yp = pool_branch(p, pool_w[l], pool_scale[l], pool_proj[l])
        x = x + m_lat[2] * merge(yf, yp, ya, gates, w_out[l])

        if not last:
            Lc = ctx.shape[1]
            qc = rmsnorm(qc, q_norm[l]).reshape(B, Lc, N_KV_HEADS, GQA_GROUP, HEAD_DIM)
            yac = attend(qc, kc, vc).reshape(B, Lc, Q_WIDTH) @ attn_proj[l]
            yfc = fourier_branch(fc, fourier_w[l])
            ypc = pool_branch(pc, pool_w[l], pool_scale[l], pool_proj[l])
            ctx = ctx + m_ctx[2] * merge(yfc, ypc, yac, gates_c, w_out[l])

        h2 = modulate(rmsnorm(x, norm_ffn[l]), m_lat[3], m_lat[4])
        x = x + m_lat[5] * peer(h2, peer_wq[l], peer_keys[l], peer_u[l], peer_v[l])
        if not last:
            hc2 = modulate(rmsnorm(ctx, norm_ffn[l]), m_ctx[3], m_ctx[4])
            ctx = ctx + m_ctx[5] * peer(hc2, peer_wq[l], peer_keys[l], peer_u[l], peer_v[l])
    return x
```

```python
import numpy as np
import ml_dtypes
from contextlib import ExitStack
import concourse.bass as bass
import concourse.mybir as mybir
from concourse.bass_utils import run_bass_kernel_spmd

F32 = mybir.dt.float32
BF = mybir.dt.bfloat16
U32 = mybir.dt.uint32
AF = mybir.ActivationFunctionType
ALU = mybir.AluOpType
AX = mybir.AxisListType

D = 1024
L = 2048
LC = 256
LT = L + LC
NT = 18
DEPTH = 4
EPS = 1e-6
NEXP = 16384
NEG = -1.0e30
KCUT = 99
KTILES = 99
import os
KPT = int(os.environ.get('KPT', '99'))


class Sch:
    NDS = 24
    NHW = 16

    def __init__(s, nc, es):
        s.nc = nc
        s.E = {'pe': nc.tensor, 'act': nc.scalar, 'dve': nc.vector, 'pool': nc.gpsimd, 'sp': nc.sync}
        s.sem = {k: es.enter_context(nc.semaphore('sem_' + k)) for k in ('pe', 'act', 'dve', 'pool')}
        s.cnt = {k: 0 for k in s.sem}
        s.dsem = [es.enter_context(nc.semaphore('dsem%d' % i)) for i in range(s.NDS)]
        s.dcnt = [0] * s.NDS
        s.dn = 0
        s.dnsw = 0
        s.waited = {}
        s.lw = {}
        s.rd = {}
        s.ninst = 0

    def _semobj(s, n):
        return s.sem[n] if isinstance(n, str) else s.dsem[n]

    def _wait(s, eng, ev):
        n, v = ev
        if s.waited.get((eng, n), 0) >= v:
            return
        s.E[eng].wait_ge(s._semobj(n), v)
        s.waited[(eng, n)] = v

    def _deps(s, eng, reads, writes):
        evs = []
        for k in reads:
            if k in s.lw:
                evs.append(s.lw[k])
        for k in writes:
            if k in s.lw:
                evs.append(s.lw[k])
            evs.extend(s.rd.get(k, {}).items())
        for ev in evs:
            if eng == 'pe' and ev[0] == 'pe':
                continue
            s._wait(eng, ev)

    def _record(s, ev, reads, writes):
        for k in reads:
            d = s.rd.setdefault(k, {})
            d[ev[0]] = max(d.get(ev[0], 0), ev[1])
        for k in writes:
            s.lw[k] = ev
            s.rd[k] = {}

    @staticmethod
    def _isps(k):
        n = k if isinstance(k, str) else k[0]
        return isinstance(n, str) and n.startswith('ps')

    def op(s, eng, fn, reads=(), writes=()):
        writes = list(writes) + [k for k in reads if s._isps(k)]
        reads = [k for k in reads if not s._isps(k)]
        s._deps(eng, reads, writes)
        ins = fn(s.E[eng])
        s.cnt[eng] += 1
        ins.then_inc(s.sem[eng], 1)
        ev = (eng, s.cnt[eng])
        s._record(ev, reads, writes)
        s.ninst += 1
        return ev

    def _pick(s, q):
        if q == 'pool':
            i = s.NHW + s.dnsw
            s.dnsw = (s.dnsw + 1) % (s.NDS - s.NHW)
        else:
            i = s.dn
            s.dn = (s.dn + 1) % s.NHW
        return i

    def dma(s, q, out, in_, reads=(), writes=(), **kw):
        i = s._pick(q)
        if s.dcnt[i]:
            s._wait(q, (i, s.dcnt[i]))
        s._deps(q, reads, writes)
        ins = s.E[q].dma_start(out=out, in_=in_, **kw)
        ins.then_inc(s.dsem[i], 16)
        s.dcnt[i] += 16
        ev = (i, s.dcnt[i])
        s._record(ev, reads, writes)
        s.ninst += 1
        return ev

    def gather(s, out, table, idx, reads=(), writes=()):
        q = 'pool'
        i = s._pick(q)
        if s.dcnt[i]:
            s._wait(q, (i, s.dcnt[i]))
        s._deps(q, reads, writes)
        ins = s.E[q].indirect_dma_start(out=out, out_offset=None, in_=table,
                                        in_offset=bass.IndirectOffsetOnAxis(ap=idx, axis=0))
        ins.then_inc(s.dsem[i], 16)
        s.dcnt[i] += 16
        ev = (i, s.dcnt[i])
        s._record(ev, reads, writes)
        s.ninst += 1
        return ev

    def barrier(s):
        evs = [(k, s.cnt[k]) for k in s.sem if s.cnt[k]] + [(i, c) for i, c in enumerate(s.dcnt) if c]
        for eng in s.E:
            for ev in evs:
                s._wait(eng, ev)
        s.lw = {}
        s.rd = {}


def build(depth=DEPTH, dbg=False, stop_after=None, all_ctx=False, WD=DEPTH, xr_out=False, peer_only=False):
    nc = bass.Bass("TRN2", target_bir_lowering=False)

    def din(name, shape, dt=F32):
        return nc.dram_tensor(name, list(shape), dt, kind="ExternalInput").ap()

    def dsc(name, shape, dt=F32):
        return nc.dram_tensor(name, list(shape), dt, kind=("ExternalOutput" if dbg else "Internal")).ap()

    x_in = din("x", [2, L, D])
    ctx_in = din("ctx", [2, LC, D])
    cct = din("cct", [128, 24])
    ada_w = din("ada_w", [WD, D, 6 * D])
    ada_b = din("ada_b", [WD, 6 * D])
    norm_mix = din("norm_mix", [WD, D])
    w_in = din("w_in", [WD, D, 4352])
    fourier_w = din("fourier_w", [WD, 256, D])
    pool_w = din("pool_w", [WD, 4, 64, 64])
    pool_scale = din("pool_scale", [WD, 256])
    pool_proj = din("pool_proj", [WD, 256, D])
    q_norm = din("q_norm", [WD, 64])
    k_norm = din("k_norm", [WD, 64])
    attn_proj = din("attn_proj", [WD, 512, D])
    w_out = din("w_out", [WD, D, D])
    norm_ffn = din("norm_ffn", [WD, D])
    peer_wq = din("peer_wq", [WD, D, 2048])
    peer_keys = din("peer_keys", [WD, 8, 2, 128, 128])
    need_peer = stop_after is None or stop_after >= 7 or peer_only
    peer_u = din("peer_u", [WD, NEXP, D]) if need_peer else None
    peer_v = din("peer_v", [WD, NEXP, D]) if need_peer else None
    dftc = din("dftc", [L, L], BF)
    dfts = din("dfts", [L, L], BF)
    dftc2 = din("dftc2", [LC, LC], BF)
    dfts2 = din("dfts2", [LC, LC], BF)
    chd_d = din("chd", [128, 256], BF)
    ropec = din("ropec", [L, 32])
    ropes = din("ropes", [L, 32])
    invl_d = din("invl", [2, 128, L])
    invx_d = din("invx", [2, 128, LC])
    identb_d = din("identb", [128, 128], BF)
    identf_d = din("identf", [128, 128])
    iota_d = din("iota16", [128, 16])

    y = None if xr_out else nc.dram_tensor("y", [2, L, D], F32, kind="ExternalOutput").ap()

    xr = nc.dram_tensor("xr", [2, LT, D], F32, kind="ExternalOutput").ap() if xr_out else (din("xr_in", [2, LT, D]) if peer_only else dsc("xr", [2, LT, D]))
    xo = nc.dram_tensor("xo", [2, LT, D], F32, kind="ExternalOutput").ap() if peer_only else None
    hT_d = dsc("hT_d", [2, 8, 128, LT], BF)
    XCS_d = dsc("XCS_d", [2, NT, 128, 512], BF)
    pT_d = dsc("pT_d", [2, 2, 128, LT])
    qT_d = dsc("qT_d", [2, 64, 8, LT], BF)
    kT_d = dsc("kT_d", [2, 64, 2, LT], BF)
    v1_d = dsc("v1_d", [2, 128, NT, 130], BF)
    ZT_d = dsc("ZT_d", [2, 2, 128, LT], BF)
    plT_d = dsc("plT_d", [2, 2, 128, LT], BF)
    yaT_d = dsc("yaT_d", [2, 4, 128, LT], BF)
    modv_d = din("modv_in", [WD, 3, 6, D]) if peer_only else dsc("modv_d", [WD, 3, 6, D])
    eidx_d = dsc("eidx_d", [36, 128, 128], U32) if dbg else None
    ub_d = nc.dram_tensor("ub_d", [NEXP, D], BF, kind="Internal").ap()
    vb_d = nc.dram_tensor("vb_d", [NEXP, D], BF, kind="Internal").ap()

    def xsrc(l, s, i):
        if l == 0:
            if i < 16:
                return x_in[s, i * 128:(i + 1) * 128, :]
            return ctx_in[s, (i - 16) * 128:(i - 15) * 128, :]
        return xr[s, i * 128:(i + 1) * 128, :]

    with ExitStack() as es:
        S = Sch(nc, es)

        def run_phase(fn, *a):
            with ExitStack() as ph:
                cnt = [0]

                def T(shape, dt=F32, name=None):
                    cnt[0] += 1
                    return ph.enter_context(nc.sbuf_tensor(name or ("t%d_%d" % (S.ninst, cnt[0])), list(shape), dt))

                def P(shape, dt=F32, name=None):
                    cnt[0] += 1
                    return ph.enter_context(nc.psum_tensor(name or ("p%d_%d" % (S.ninst, cnt[0])), list(shape), dt))

                fn(T, P, *a)
                S.barrier()

        castrr = [0]

        def cast(out, in_, reads, writes):
            e = ('dve', 'act')[castrr[0] % 2]
            castrr[0] += 1
            if e == 'act':
                S.op('act', lambda g: g.copy(out=out, in_=in_), reads, writes)
            else:
                S.op(e, lambda g: g.tensor_copy(out=out, in_=in_), reads, writes)

        def load_w_bf(T, dst, dkey, src_rows, ncols, stg, nk):
            cw = stg[0].shape[-1]
            n = 0
            for k in range(nk):
                for c0 in range(0, ncols, cw):
                    c1 = min(ncols, c0 + cw)
                    j = n % 2
                    n += 1
                    S.dma('sp', stg[j][:, 0:c1 - c0], src_rows(k)[:, c0:c1], writes=[('stg', id(stg), j)])
                    cast(dst[:, k, c0:c1], stg[j][:, 0:c1 - c0], [('stg', id(stg), j)], [dkey])

        def ph_mod(T, P, l):
            cc = T([128, 24])
            sc = T([128, 24])
            adab = T([3, 6 * D])
            mods = T([3, 6 * D])
            gm = T([3, D])
            gf = T([3, D])
            mo = T([3, 6, D])
            wst = [T([128, 8, 512]) for _ in range(2)]
            ps = [P([128, 512]) for _ in range(2)]
            S.dma('sp', cc[:], cct, writes=['cc'])
            S.dma('sp', adab[:], ada_b[l:l + 1, :].broadcast_to([3, 6 * D]), writes=['adab'])
            S.dma('sp', gm[:], norm_mix[l:l + 1, :].broadcast_to([3, D]), writes=['gm'])
            S.dma('sp', gf[:], norm_ffn[l:l + 1, :].broadcast_to([3, D]), writes=['gf'])
            S.op('act', lambda g: g.activation(out=sc[:], in_=cc[:], func=AF.Silu), ['cc'], ['sc'])
            for nb in range(12):
                j = nb % 2
                for k2 in range(2):
                    S.dma('sp', wst[j][:, k2 * 4:(k2 + 1) * 4, :],
                          ada_w[l, k2 * 512:(k2 + 1) * 512, nb * 512:(nb + 1) * 512].rearrange("(k p) n -> p k n", p=128),
                          writes=[('wst', j)])
                for k in range(8):
                    S.op('pe', lambda g: g.matmul(ps[j][0:3, :], lhsT=sc[:, k * 3:(k + 1) * 3], rhs=wst[j][:, k, :],
                                                  start=(k == 0), stop=(k == 7)),
                         ['sc', ('wst', j)], [('psm', j)])
                S.op('dve', lambda g: g.tensor_tensor(out=mods[:, nb * 512:(nb + 1) * 512], in0=ps[j][0:3, :],
                                                      in1=adab[:, nb * 512:(nb + 1) * 512], op=ALU.add),
                     [('psm', j), 'adab'], ['mods'])
            S.op('dve', lambda g: g.scalar_tensor_tensor(out=mo[:, 0, :], in0=mods[:, D:2 * D], scalar=1.0, op0=ALU.add,
                                                         in1=gm[:], op1=ALU.mult), ['mods', 'gm'], ['mo'])
            S.op('dve', lambda g: g.scalar_tensor_tensor(out=mo[:, 3, :], in0=mods[:, 4 * D:5 * D], scalar=1.0, op0=ALU.add,
                                                         in1=gf[:], op1=ALU.mult), ['mods', 'gf'], ['mo'])
            for dst, src in ((1, 0), (2, 2), (4, 3), (5, 5)):
                S.op('dve', lambda g: g.tensor_copy(out=mo[:, dst, :], in_=mods[:, src * D:(src + 1) * D]), ['mods'], ['mo'])
            S.dma('sp', modv_d[l], mo[:], reads=['mo'])

        def ph_m1(T, P, l):
            W1 = T([128, 8, 1280], BF)
            stg = [T([128, 1280]) for _ in range(2)]
            identb = T([128, 128], BF)
            chd = T([128, 256], BF)
            gq = T([128, 640])
            rc = T([128, 16, 32])
            rs = T([128, 16, 32])
            modb = [[T([128, D]) for _ in range(2)] for _ in range(3)]
            xt = [T([128, D]) for _ in range(2)]
            junk = T([128, D])
            tmp = T([128, D])
            hb = T([128, D], BF)
            hT = T([128, 8, 128], BF)
            fT = T([128, 2, 128], BF)
            xcs = T([128, 512], BF)
            pt = T([128, 2, 128])
            sqq = T([128, 640])
            qn = T([128, 640])
            ta = T([128, 320])
            tb = T([128, 320])
            tcx = T([128, 320])
            td = T([128, 320])
            qr = T([128, 640], BF)
            qTs = T([64, 8, 128], BF)
            kTs = T([64, 2, 128], BF)
            v1 = [T([128, 130], BF) for _ in range(2)]
            ss = T([128, 4])
            ssq = T([128, 10])
            rq = T([128, 10])
            psq = P([128, 1024])
            psT = P([128, 8, 128], BF)
            psfp = P([128, 4, 128])
            psx = P([128, 512])
            psqT = P([64, 8, 128], BF)
            pskT = P([64, 8, 128], BF)
            psv = P([128, 512])

            S.dma('sp', identb[:], identb_d, writes=['identb'])
            S.dma('sp', chd[:], chd_d, writes=['chd'])
            S.dma('sp', gq[:, 0:512].rearrange("p (h d) -> p h d", d=64),
                  bass.AP(tensor=q_norm.tensor, offset=l * 64, ap=[[0, 128], [0, 8], [1, 64]]),
                  writes=['gq'])
            S.dma('sp', gq[:, 512:640].rearrange("p (h d) -> p h d", d=64),
                  bass.AP(tensor=k_norm.tensor, offset=l * 64, ap=[[0, 128], [0, 2], [1, 64]]),
                  writes=['gq'])
            for q4 in range(4):
                S.dma('sp', rc[:, q4 * 4:(q4 + 1) * 4, :], ropec[q4 * 512:(q4 + 1) * 512, :].rearrange("(t p) c -> p t c", p=128),
                      writes=['rope'])
                S.dma('sp', rs[:, q4 * 4:(q4 + 1) * 4, :], ropes[q4 * 512:(q4 + 1) * 512, :].rearrange("(t p) c -> p t c", p=128),
                      writes=['rope'])
            for r in range(3):
                for j in range(2):
                    S.dma('sp', modb[r][j][:], modv_d[l, r, j:j + 1, :].broadcast_to([128, D]), writes=['modb'])
            for j in range(2):
                S.op('dve', lambda g: g.memset(v1[j][:], 1.0), (), [('v1', j)])
            load_w_bf(T, W1, 'W1', lambda k: w_in[l, k * 128:(k + 1) * 128, 0:1280], 1280, stg, 8)

            tiles = [(s, i) for s in range(2) for i in range(NT)][:KTILES]
            S.dma('sp', xt[0][:], xsrc(l, 0, 0), writes=[('xt', 0)])
            for n, (s, i) in enumerate(tiles):
                j = n % 2
                if n + 1 < len(tiles):
                    s2, i2 = tiles[n + 1]
                    S.dma('sp', xt[1 - j][:], xsrc(l, s2, i2), writes=[('xt', 1 - j)])
                r = s if i < 16 else 2
                t0 = i * 128
                X = xt[j]
                S.op('act', lambda g: g.activation(out=junk[:], in_=X[:], func=AF.Square), [('xt', j)], ['junk'])
                S.op('dve', lambda g: g.tensor_reduce(out=ss[:, 0:1], in_=junk[:], axis=AX.X, op=ALU.add), ['junk'], ['ss'])
                S.op('dve', lambda g: g.tensor_scalar(out=ss[:, 1:2], in0=ss[:, 0:1], scalar1=1.0 / D, scalar2=EPS,
                                                      op0=ALU.mult, op1=ALU.add), ['ss'], ['ss'])
                S.op('act', lambda g: g.activation(out=ss[:, 2:3], in_=ss[:, 1:2], func=AF.Sqrt), ['ss'], ['ss'])
                S.op('dve', lambda g: g.reciprocal(out=ss[:, 3:4], in_=ss[:, 2:3]), ['ss'], ['ss'])
                S.op('dve', lambda g: g.scalar_tensor_tensor(out=tmp[:], in0=X[:], scalar=ss[:, 3:4], op0=ALU.mult,
                                                             in1=modb[r][0][:], op1=ALU.mult),
                     [('xt', j), 'ss', 'modb'], ['tmp'])
                S.op('dve', lambda g: g.tensor_tensor(out=hb[:], in0=tmp[:], in1=modb[r][1][:], op=ALU.add),
                     ['tmp', 'modb'], ['hb'])
                for k in range(8):
                    S.op('pe', lambda g: g.transpose(out=psT[:, k, :], in_=hb[:, k * 128:(k + 1) * 128], identity=identb[:]),
                         ['hb', 'identb'], ['psT'])
                S.op('act', lambda g: g.copy(out=hT[:], in_=psT[:]), ['psT'], ['hT'])
                for k2 in range(2):
                    S.dma('act', hT_d[s, k2 * 4:(k2 + 1) * 4, :, t0:t0 + 128].rearrange("k p t -> p k t"),
                          hT[:, k2 * 4:(k2 + 1) * 4, :], reads=['hT'])
                if KCUT <= 1:
                    continue
                for c in range(4):
                    for k in range(8):
                        S.op('pe', lambda g: g.matmul(psfp[:, c, :], lhsT=W1[:, k, c * 128:(c + 1) * 128], rhs=hT[:, k, :],
                                                      start=(k == 0), stop=(k == 7)), ['W1', 'hT'], ['psfp'])
                S.op('dve', lambda g: g.tensor_copy(out=fT[:], in_=psfp[:, 0:2, :]), ['psfp'], ['fT'])
                S.op('act', lambda g: g.copy(out=pt[:], in_=psfp[:, 2:4, :]), ['psfp'], ['pt'])
                S.dma('act', pT_d[s, :, :, t0:t0 + 128].rearrange("c p t -> p c t"), pt[:], reads=['pt'])
                if KCUT <= 2:
                    continue
                for c in range(2):
                    S.op('pe', lambda g: g.matmul(psx[:, c * 256:(c + 1) * 256], lhsT=fT[:, c, :], rhs=chd[:],
                                                  start=True, stop=True), ['fT', 'chd'], ['psx'])
                S.op('act', lambda g: g.copy(out=xcs[:], in_=psx[:]), ['psx'], ['xcs'])
                S.dma('act', XCS_d[s, i], xcs[:], reads=['xcs'])
                if KCUT <= 3:
                    continue
                for k in range(8):
                    S.op('pe', lambda g: g.matmul(psq[:, 0:512], lhsT=hT[:, k, :], rhs=W1[:, k, 512:1024],
                                                  start=(k == 0), stop=(k == 7)), ['W1', 'hT'], ['psq0'])
                for k in range(8):
                    S.op('pe', lambda g: g.matmul(psq[:, 512:640], lhsT=hT[:, k, :], rhs=W1[:, k, 1024:1152],
                                                  start=(k == 0), stop=(k == 7)), ['W1', 'hT'], ['psq1'])
                for k in range(8):
                    S.op('pe', lambda g: g.matmul(psv[:, 0:128], lhsT=hT[:, k, :], rhs=W1[:, k, 1152:1280],
                                                  start=(k == 0), stop=(k == 7)), ['W1', 'hT'], ['psv'])
                S.op('act', lambda g: g.activation(out=sqq[:], in_=psq[:, 0:640], func=AF.Square), ['psq0', 'psq1'], ['sqq'])
                S.op('dve', lambda g: g.tensor_reduce(out=ssq[:], in_=sqq[:].rearrange("p (h d) -> p h d", d=64),
                                                      axis=AX.X, op=ALU.add), ['sqq'], ['ssq'])
                S.op('dve', lambda g: g.tensor_scalar(out=ssq[:], in0=ssq[:], scalar1=1.0 / 64, scalar2=EPS,
                                                      op0=ALU.mult, op1=ALU.add), ['ssq'], ['ssq'])
                S.op('act', lambda g: g.activation(out=ssq[:], in_=ssq[:], func=AF.Sqrt), ['ssq'], ['ssq'])
                S.op('dve', lambda g: g.reciprocal(out=rq[:], in_=ssq[:]), ['ssq'], ['rq'])
                S.op('dve', lambda g: g.tensor_tensor(out=qn[:].rearrange("p (h d) -> p h d", d=64),
                                                      in0=psq[:, 0:640].rearrange("p (h d) -> p h d", d=64),
                                                      in1=rq[:].unsqueeze(2).broadcast_to([128, 10, 64]), op=ALU.mult),
                     ['psq0', 'psq1', 'rq'], ['qn'])
                if KCUT <= 4:
                    continue
                if i < 16:
                    S.op('dve', lambda g: g.tensor_tensor(out=qn[:], in0=qn[:], in1=gq[:], op=ALU.mult), ['qn', 'gq'], ['qn'])
                    qv = qn[:].rearrange("p (h a b f) -> p h a b f", h=10, a=2, b=2)
                    qo = qr[:].rearrange("p (h a b f) -> p h a b f", h=10, a=2, b=2)
                    x1, x2 = qv[:, :, :, 0, :], qv[:, :, :, 1, :]
                    cosb = rc[:, i, :].rearrange("p (a f) -> p a f", a=2).unsqueeze(1).broadcast_to([128, 10, 2, 16])
                    sinb = rs[:, i, :].rearrange("p (a f) -> p a f", a=2).unsqueeze(1).broadcast_to([128, 10, 2, 16])
                    v4 = lambda t: t[:].rearrange("p (h a f) -> p h a f", h=10, a=2)
                    S.op('dve', lambda g: g.tensor_tensor(out=v4(ta), in0=x1, in1=cosb, op=ALU.mult), ['qn', 'rope'], ['ta'])
                    S.op('dve', lambda g: g.tensor_tensor(out=v4(tb), in0=x2, in1=sinb, op=ALU.mult), ['qn', 'rope'], ['tb'])
                    S.op('dve', lambda g: g.tensor_tensor(out=v4(tcx), in0=x2, in1=cosb, op=ALU.mult), ['qn', 'rope'], ['tcx'])
                    S.op('dve', lambda g: g.tensor_tensor(out=v4(td), in0=x1, in1=sinb, op=ALU.mult), ['qn', 'rope'], ['td'])
                    S.op('dve', lambda g: g.tensor_tensor(out=qo[:, :, :, 0, :], in0=v4(ta), in1=v4(tb), op=ALU.subtract),
                         ['ta', 'tb'], ['qr'])
                    S.op('dve', lambda g: g.tensor_tensor(out=qo[:, :, :, 1, :], in0=v4(tcx), in1=v4(td), op=ALU.add),
                         ['tcx', 'td'], ['qr'])
                else:
                    S.op('dve', lambda g: g.tensor_tensor(out=qr[:], in0=qn[:], in1=gq[:], op=ALU.mult), ['qn', 'gq'], ['qr'])
                if KCUT <= 5:
                    continue
                for h in range(8):
                    S.op('pe', lambda g: g.transpose(out=psqT[:, h, :], in_=qr[:, h * 64:(h + 1) * 64], identity=identb[:]),
                         ['qr', 'identb'], ['psqT'])
                for h in range(2):
                    S.op('pe', lambda g: g.transpose(out=pskT[:, h, :], in_=qr[:, 512 + h * 64:512 + (h + 1) * 64],
                                                     identity=identb[:]), ['qr', 'identb'], ['pskT'])
                S.op('dve', lambda g: g.tensor_copy(out=qTs[:], in_=psqT[:]), ['psqT'], ['qTs'])
                S.op('act', lambda g: g.copy(out=kTs[:], in_=pskT[:, 0:2, :]), ['pskT'], ['kTs'])
                S.dma('act', qT_d[s, :, :, t0:t0 + 128], qTs[:], reads=['qTs'])
                S.dma('act', kT_d[s, :, :, t0:t0 + 128], kTs[:], reads=['kTs'])
                V = v1[j]
                S.op('dve', lambda g: g.tensor_copy(out=V[:].rearrange("p (g e) -> p g e", g=2)[:, :, 0:64],
                                                    in_=psv[:, 0:128].rearrange("p (g e) -> p g e", g=2)), ['psv'], [('v1', j)])
                S.dma('act', v1_d[s, :, i, :], V[:], reads=[('v1', j)])

        def ph_f(T, P, l, last):
            xcs = T([128, 2, NT, 512], BF)
            cl = [T([128, 16, 512], BF) for _ in range(2)]
            sl = [T([128, 16, 512], BF) for _ in range(2)]
            c2 = T([128, 2, 256], BF)
            s2 = T([128, 2, 256], BF)
            zt = [T([128, 512], BF) for _ in range(2)]
            psZ = [P([128, 512]) for _ in range(2)]
            for s in range(2):
                for t3 in range(0, NT, 3):
                    S.dma('sp', xcs[:, s, t3:t3 + 3, :], XCS_d[s, t3:t3 + 3].rearrange("t p c -> p t c"), writes=['xcs'])
            S.dma('sp', c2[:], dftc2.rearrange("(k p) n -> p k n", p=128), writes=['c2'])
            S.dma('sp', s2[:], dfts2.rearrange("(k p) n -> p k n", p=128), writes=['c2'])
            n = 0
            for pb in range(4):
                jb = pb % 2
                for k4 in range(4):
                    S.dma('sp', cl[jb][:, k4 * 4:(k4 + 1) * 4, :],
                          dftc[k4 * 512:(k4 + 1) * 512, pb * 512:(pb + 1) * 512].rearrange("(k p) n -> p k n", p=128),
                          writes=[('cl', jb)])
                    S.dma('sp', sl[jb][:, k4 * 4:(k4 + 1) * 4, :],
                          dfts[k4 * 512:(k4 + 1) * 512, pb * 512:(pb + 1) * 512].rearrange("(k p) n -> p k n", p=128),
                          writes=[('cl', jb)])
                for s in range(2):
                    for c in range(2):
                        j = n % 2
                        n += 1
                        for kt in range(16):
                            S.op('pe', lambda g: g.matmul(psZ[j][:], lhsT=xcs[:, s, kt, c * 256:c * 256 + 128],
                                                          rhs=cl[jb][:, kt, :], start=(kt == 0), stop=False),
                                 ['xcs', ('cl', jb)], [('psZ', j)])
                            S.op('pe', lambda g: g.matmul(psZ[j][:], lhsT=xcs[:, s, kt, c * 256 + 128:c * 256 + 256],
                                                          rhs=sl[jb][:, kt, :], start=False, stop=(kt == 15)),
                                 ['xcs', ('cl', jb)], [('psZ', j)])
                        S.op('act', lambda g: g.activation(out=zt[j][:], in_=psZ[j][:], func=AF.Copy,
                                                           scale=float((L * 64) ** -0.5)), [('psZ', j)], [('zt', j)])
                        S.dma('act', ZT_d[s, c, :, pb * 512:(pb + 1) * 512], zt[j][:], reads=[('zt', j)])
            if not last:
                for s in range(2):
                    for c in range(2):
                        j = n % 2
                        n += 1
                        for kt in range(2):
                            S.op('pe', lambda g: g.matmul(psZ[j][:, 0:256], lhsT=xcs[:, s, 16 + kt, c * 256:c * 256 + 128],
                                                          rhs=c2[:, kt, :], start=(kt == 0), stop=False),
                                 ['xcs', 'c2'], [('psZ', j)])
                            S.op('pe', lambda g: g.matmul(psZ[j][:, 0:256],
                                                          lhsT=xcs[:, s, 16 + kt, c * 256 + 128:c * 256 + 256],
                                                          rhs=s2[:, kt, :], start=False, stop=(kt == 1)),
                                 ['xcs', 'c2'], [('psZ', j)])
                        S.op('act', lambda g: g.activation(out=zt[j][:, 0:256], in_=psZ[j][:, 0:256], func=AF.Copy,
                                                           scale=float((LC * 64) ** -0.5)), [('psZ', j)], [('zt', j)])
                        S.dma('act', ZT_d[s, c, :, L:LT], zt[j][:, 0:256], reads=[('zt', j)])

        def ph_p(T, P, l, last):
            invl = T([128, 2, L])
            invx = T([128, 2, LC])
            PL = T([128, L + 32])
            PX = T([128, LC + 32])
            A = T([128, L + 32])
            B = T([128, L + 32])
            tmp = T([128, L])
            po = T([128, L], BF)
            S.dma('sp', invl[:], invl_d.rearrange("c p t -> p c t"), writes=['inv'])
            S.dma('sp', invx[:], invx_d.rearrange("c p t -> p c t"), writes=['inv'])
            S.op('dve', lambda g: g.memset(PL[:], 0.0), (), ['PL'])
            S.op('dve', lambda g: g.memset(PX[:], 0.0), (), ['PX'])
            for s in range(2):
                for kind in range(1 if last else 2):
                    for c in range(2):
                        if kind == 0:
                            Pb, pk, n, t0, inv = PL, 'PL', L, 0, invl
                        else:
                            Pb, pk, n, t0, inv = PX, 'PX', LC, L, invx
                        S.dma('sp', Pb[:, 16:16 + n], pT_d[s, c, :, t0:t0 + n], writes=[pk])
                        tt = lambda e, o, a, b, rd, wr: S.op(e, lambda g: g.tensor_tensor(out=o, in0=a, in1=b, op=ALU.add), rd, wr)
                        tt('dve', A[:, 1:n + 31], Pb[:, 1:n + 31], Pb[:, 0:n + 30], [pk], ['A'])
                        tt('dve', B[:, 2:n + 30], A[:, 3:n + 31], A[:, 1:n + 29], ['A'], ['B'])
                        if c == 1:
                            tt('dve', A[:, 4:n + 28], B[:, 6:n + 30], B[:, 2:n + 26], ['B'], ['A'])
                            tt('dve', B[:, 8:n + 24], A[:, 12:n + 28], A[:, 4:n + 20], ['A'], ['B'])
                        S.op('dve', lambda g: g.tensor_tensor(out=tmp[0:64, 0:n], in0=A[0:64, 16:16 + n], in1=inv[0:64, c, :],
                                                              op=ALU.mult), ['A', 'inv'], ['tmp0'])
                        S.op('dve', lambda g: g.tensor_tensor(out=tmp[64:128, 0:n], in0=B[64:128, 16:16 + n],
                                                               in1=inv[64:128, c, :], op=ALU.mult), ['B', 'inv'], ['tmp1'])
                        S.op('dve', lambda g: g.tensor_tensor(out=po[:, 0:n], in0=tmp[:, 0:n], in1=Pb[:, 16:16 + n],
                                                              op=ALU.subtract), ['tmp0', 'tmp1', pk], ['po'])
                        S.dma('act', plT_d[s, c, :, t0:t0 + n], po[:, 0:n], reads=['po'])

        def ph_a(T, P, l, last):
            identb = T([128, 128], BF)
            kT = [T([128, 2, LT], BF) for _ in range(2)]
            qT = [T([128, 8, LT], BF) for _ in range(2)]
            v1 = [T([128, NT, 130], BF) for _ in range(2)]
            Pm = [T([128, 512], BF) for _ in range(3)]
            ya = [T([128, 512], BF) for _ in range(2)]
            yaT = [T([128, 4, 128], BF) for _ in range(2)]
            rden = T([128, 8])
            acc = P([128, 4, 512])
            psS = [P([128, 512]) for _ in range(2)]
            psY = P([128, 4, 128], BF)
            S.dma('sp', identb[:], identb_d, writes=['identb'])
            for s in range(2):
                S.op('dve', lambda g: g.memset(kT[s][64:128], 0.0), (), [('kT', s)])
                S.op('dve', lambda g: g.memset(qT[s][64:128], 0.0), (), [('qT', s)])
                S.dma('sp', kT[s][0:64], kT_d[s], writes=[('kT', s)])
                S.dma('sp', v1[s][:], v1_d[s], writes=[('v1', s)])
                S.dma('sp', qT[s][0:64], qT_d[s], writes=[('qT', s)])
            n = 0
            nq = 0
            for s in range(2):
                jobs = [(qt, list(range(NT))) for qt in range(16)]
                if not last:
                    jobs += [(qt, [16, 17]) for qt in (16, 17)]
                for qt, kts in jobs:
                    jq = nq % 2
                    nq += 1
                    for gi in range(2):
                        for ki, kt in enumerate(kts):
                            j = n % 2
                            j3 = n % 3
                            n += 1
                            S.op('pe', lambda g: g.matmul(psS[j][:], lhsT=kT[s][:, gi, kt * 128:(kt + 1) * 128],
                                                          rhs=qT[s][:, 4 * gi:4 * gi + 4, qt * 128:(qt + 1) * 128],
                                                          start=True, stop=True),
                                 [('kT', s), ('qT', s)], [('psS', j)])
                            S.op('act', lambda g: g.activation(out=Pm[j3][:], in_=psS[j][:], func=AF.Exp, scale=0.125),
                                 [('psS', j)], [('Pm', j3)])
                            for hh in range(4):
                                S.op('pe', lambda g: g.matmul(acc[:, hh, 0:65], lhsT=Pm[j3][:, hh * 128:(hh + 1) * 128],
                                                              rhs=v1[s][:, kt, gi * 65:(gi + 1) * 65],
                                                              start=(ki == 0), stop=(ki == len(kts) - 1)),
                                     [('Pm', j3), ('v1', s)], ['psacc'])
                        S.op('dve', lambda g: g.reciprocal(out=rden[:, 4 * gi:4 * gi + 4], in_=acc[:, :, 64]), ['psacc'], ['rden'])
                        S.op('dve', lambda g: g.tensor_tensor(
                            out=ya[jq][:, gi * 256:(gi + 1) * 256].rearrange("p (h d) -> p h d", d=64),
                            in0=acc[:, :, 0:64],
                            in1=rden[:, 4 * gi:4 * gi + 4].unsqueeze(2).broadcast_to([128, 4, 64]), op=ALU.mult),
                            ['psacc', 'rden'], [('ya', jq)])
                    for c in range(4):
                        S.op('pe', lambda g: g.transpose(out=psY[:, c, :], in_=ya[jq][:, c * 128:(c + 1) * 128],
                                                         identity=identb[:]), [('ya', jq), 'identb'], ['psY'])
                    S.op('dve', lambda g: g.tensor_copy(out=yaT[jq][:], in_=psY[:]), ['psY'], [('yaT', jq)])
                    S.dma('sp', yaT_d[s, :, :, qt * 128:(qt + 1) * 128].rearrange("c p t -> p c t"), yaT[jq][:],
                          reads=[('yaT', jq)])

        def ph_m3(T, P, l, last):
            WG = T([128, 8, 3072], BF)
            fw = T([128, 2, D], BF)
            ppj = T([128, 2, D], BF)
            apj = T([128, 4, D], BF)
            wo = T([128, 8, D], BF)
            pwf = T([128, 2, 128])
            pwbd = T([128, 2, 128], BF)
            psc = T([128, 2])
            stg = [T([128, 1536]) for _ in range(2)]
            g2b = [T([128, D]) for _ in range(3)]
            hTb = [T([128, 8, 512], BF) for _ in range(2)]
            ZTb = [T([128, 2, 512], BF) for _ in range(2)]
            plb = [T([128, 2, 512], BF) for _ in range(2)]
            yab = [T([128, 4, 512], BF) for _ in range(2)]
            mx = T([128, 2, 512], BF)
            sg = [T([128, 512]) for _ in range(3)]
            tq = [T([128, 512]) for _ in range(3)]
            uT = T([128, 8, 512], BF)
            xt = [T([128, D]) for _ in range(2)]
            tmo = T([128, D])
            xn = [T([128, D]) for _ in range(2)]
            psM = P([128, 512])
            psA = P([128, 512])
            psB = P([128, 512])
            psC = P([128, 512])
            psG = [P([128, 512]) for _ in range(2)]
            psO = [P([128, 512]) for _ in range(2)]

            S.op('dve', lambda g: g.memset(pwf[:], 0.0), (), ['pwf'])
            for gidx in range(4):
                hh = gidx % 2
                S.dma('sp', pwf[hh * 64:(hh + 1) * 64, gidx // 2, hh * 64:(hh + 1) * 64], pool_w[l, gidx], writes=['pwf'])
            S.op('dve', lambda g: g.tensor_copy(out=pwbd[:], in_=pwf[:]), ['pwf'], ['pwbd'])
            S.dma('sp', psc[:], pool_scale[l].rearrange("(c p) -> p c", p=128), writes=['psc'], allow_slow_non_contiguous=True)
            for r in range(3):
                S.dma('sp', g2b[r][:], modv_d[l, r, 2:3, :].broadcast_to([128, D]), writes=['g2b'])
            load_w_bf(T, WG, 'WG', lambda k: w_in[l, k * 128:(k + 1) * 128, 1280:4352], 3072, stg, 8)
            load_w_bf(T, fw, 'fw', lambda k: fourier_w[l, k * 128:(k + 1) * 128, :], D, stg, 2)
            load_w_bf(T, ppj, 'ppj', lambda k: pool_proj[l, k * 128:(k + 1) * 128, :], D, stg, 2)
            load_w_bf(T, apj, 'apj', lambda k: attn_proj[l, k * 128:(k + 1) * 128, :], D, stg, 4)
            load_w_bf(T, wo, 'wo', lambda k: w_out[l, k * 128:(k + 1) * 128, :], D, stg, 8)

            blocks = []
            for s in range(2):
                for b in range(4):
                    blocks.append((s, b * 512, 512))
                if not last:
                    blocks.append((s, L, LC))
            ng = 0
            nx = 0
            for bi, (s, t0, nb) in enumerate(blocks):
                jb = bi % 2
                r = s if t0 < L else 2
                for k2 in range(2):
                    S.dma('sp', hTb[jb][:, k2 * 4:(k2 + 1) * 4, 0:nb],
                          hT_d[s, k2 * 4:(k2 + 1) * 4, :, t0:t0 + nb].rearrange("k p t -> p k t"), writes=[('hTb', jb)])
                S.dma('sp', ZTb[jb][:, :, 0:nb], ZT_d[s, :, :, t0:t0 + nb].rearrange("k p t -> p k t"), writes=[('ZTb', jb)])
                S.dma('sp', plb[jb][:, :, 0:nb], plT_d[s, :, :, t0:t0 + nb].rearrange("k p t -> p k t"), writes=[('plb', jb)])
                S.dma('sp', yab[jb][:, :, 0:nb], yaT_d[s, :, :, t0:t0 + nb].rearrange("k p t -> p k t"), writes=[('yab', jb)])
                for c in range(2):
                    S.op('pe', lambda g: g.matmul(psM[:, 0:nb], lhsT=pwbd[:, c, :], rhs=plb[jb][:, c, 0:nb],
                                                  start=True, stop=True), ['pwbd', ('plb', jb)], ['psM'])
                    S.op('act', lambda g: g.activation(out=mx[:, c, 0:nb], in_=psM[:, 0:nb], func=AF.Copy,
                                                       scale=psc[:, c:c + 1]), ['psM', 'psc'], ['mx'])
                for fc in range(8):
                    fs = slice(fc * 128, (fc + 1) * 128)
                    for k in range(2):
                        S.op('pe', lambda g: g.matmul(psA[:, 0:nb], lhsT=fw[:, k, fs], rhs=ZTb[jb][:, k, 0:nb],
                                                      start=(k == 0), stop=(k == 1)), ['fw', ('ZTb', jb)], ['psA'])
                    for k in range(2):
                        S.op('pe', lambda g: g.matmul(psB[:, 0:nb], lhsT=ppj[:, k, fs], rhs=mx[:, k, 0:nb],
                                                      start=(k == 0), stop=(k == 1)), ['ppj', 'mx'], ['psB'])
                    for k in range(4):
                        S.op('pe', lambda g: g.matmul(psC[:, 0:nb], lhsT=apj[:, k, fs], rhs=yab[jb][:, k, 0:nb],
                                                      start=(k == 0), stop=(k == 3)), ['apj', ('yab', jb)], ['psC'])
                    for br, pY in enumerate((psA, psB, psC)):
                        jg = ng % 2
                        ng += 1
                        for k in range(8):
                            S.op('pe', lambda g: g.matmul(psG[jg][:, 0:nb], lhsT=WG[:, k, br * D + fc * 128:br * D + (fc + 1) * 128],
                                                          rhs=hTb[jb][:, k, 0:nb], start=(k == 0), stop=(k == 7)),
                                 ['WG', ('hTb', jb)], [('psG', jg)])
                        S.op('act', lambda g: g.activation(out=sg[br][:, 0:nb], in_=psG[jg][:, 0:nb], func=AF.Sigmoid),
                             [('psG', jg)], [('sg', br)])
                        S.op('dve', lambda g: g.tensor_tensor(out=tq[br][:, 0:nb], in0=sg[br][:, 0:nb], in1=pY[:, 0:nb],
                                                              op=ALU.mult), [('sg', br), ('psA', 'psB', 'psC')[br]], [('tq', br)])
                    S.op('dve', lambda g: g.tensor_tensor(out=tq[0][:, 0:nb], in0=tq[0][:, 0:nb], in1=tq[1][:, 0:nb], op=ALU.add),
                         [('tq', 0), ('tq', 1)], [('tq', 0)])
                    S.op('dve', lambda g: g.tensor_tensor(out=uT[:, fc, 0:nb], in0=tq[0][:, 0:nb], in1=tq[2][:, 0:nb], op=ALU.add),
                         [('tq', 0), ('tq', 2)], ['uT'])
                for jt in range(nb // 128):
                    i = (t0 + jt * 128) // 128
                    jx = nx % 2
                    nx += 1
                    S.dma('sp', xt[jx][:], xsrc(l, s, i), writes=[('xt', jx)])
                    for half in range(2):
                        for k in range(8):
                            S.op('pe', lambda g: g.matmul(psO[half][:], lhsT=uT[:, k, jt * 128:(jt + 1) * 128],
                                                          rhs=wo[:, k, half * 512:(half + 1) * 512],
                                                          start=(k == 0), stop=(k == 7)), ['uT', 'wo'], [('psO', half)])
                        S.op('dve', lambda g: g.tensor_tensor(out=tmo[:, half * 512:(half + 1) * 512], in0=psO[half][:],
                                                              in1=g2b[r][:, half * 512:(half + 1) * 512], op=ALU.mult),
                             [('psO', half), 'g2b'], [('tmo', half)])
                    S.op('dve', lambda g: g.tensor_tensor(out=xn[jx][:], in0=tmo[:], in1=xt[jx][:], op=ALU.add),
                         [('tmo', 0), ('tmo', 1), ('xt', jx)], [('xn', jx)])
                    S.dma('act', xr[s, i * 128:(i + 1) * 128, :], xn[jx][:], reads=[('xn', jx)])


        def ph_conv(T, P, l):
            NR = 3
            st = [T([128, 8, D]) for _ in range(NR)]
            sb = [T([128, 8, D], BF) for _ in range(NR)]
            n = 0
            for src, dst in ((peer_u, ub_d), (peer_v, vb_d)):
                for ch in range(16):
                    j = n % NR
                    n += 1
                    rows = slice(ch * 1024, (ch + 1) * 1024)
                    S.dma('sp', st[j][:], src[l, rows, :].rearrange("(p r) d -> p r d", r=8), writes=[('cst', j)])
                    cast(sb[j][:], st[j][:], [('cst', j)], [('csb', j)])
                    S.dma('act', dst[rows, :].rearrange("(p r) d -> p r d", r=8), sb[j][:], reads=[('csb', j)])

        def ph_peer(T, P, l, last):
            NB = 10
            wq = T([128, 8, 2048], BF)
            stg = [T([128, 2048]) for _ in range(2)]
            kf = T([128, 16, 128])
            keysT = T([128, 16, 128], BF)
            identb = T([128, 128], BF)
            identf = T([128, 128])
            iota = T([128, 16])
            mbs = [[T([128, D]) for _ in range(3)] for _ in range(2)]
            xt = [T([128, D]) for _ in range(2)]
            junk = T([128, D])
            h2 = T([128, D])
            h2b = T([128, D], BF)
            h2T = T([128, 8, 128], BF)
            qTs = T([128, 16, 128], BF)
            sc = T([128, 16, 128])
            wk = T([128, 256])
            V = T([128, 16, 16])
            I = T([128, 16, 16], U32)
            If = T([128, 16, 16])
            cand = T([128, 8, 256])
            Tt = T([128, 8, 16])
            Pp = T([128, 8, 16], U32)
            pa = T([128, 8, 16], U32)
            pbb = T([128, 8, 16], U32)
            paf = T([128, 8, 16])
            pbf = T([128, 8, 16])
            eq = T([128, 8, 16, 16])
            ix1 = T([128, 8, 16])
            ix2 = T([128, 8, 16])
            ef = T([128, 128])
            eidx = [T([128, 128], U32) for _ in range(2)]
            dd = T([128, 8, 16])
            zz = T([128, 8])
            gg = T([128, 128])
            araw = T([128, 128])
            g1 = T([128, 128])
            g2 = T([128, 128])
            wgt = T([128, 128])
            ss = T([128, 4])
            Ug = [T([128, D], BF) for _ in range(NB)]
            Vg = [T([128, D], BF) for _ in range(NB)]
            uj = T([128, D], BF)
            acc = T([128, D])
            xn = [T([128, D]) for _ in range(2)]
            psQ = P([128, 16, 128])
            psV0 = None
            psT = P([128, 8, 128], BF)
            psK = P([128, 4, 128])
            psV = P([128, 1024])
            dg = [T([128, 128], BF) for _ in range(NB)]

            S.dma('sp', identb[:], identb_d, writes=['identb'])
            S.dma('sp', identf[:], identf_d, writes=['identf'])
            S.dma('sp', iota[:], iota_d, writes=['iota'])
            for h4 in range(4):
                S.dma('sp', kf[:, h4 * 4:(h4 + 1) * 4, :], peer_keys[l, 2 * h4:2 * h4 + 2].rearrange("h p n d -> n (h p) d"),
                      writes=['kf'])
            for hp in range(16):
                S.op('pe', lambda g: g.transpose(out=psK[:, hp % 4, :], in_=kf[:, hp, :], identity=identf[:]),
                     ['kf', 'identf'], ['psK'])
                if hp % 4 == 3:
                    S.op('dve', lambda g: g.tensor_copy(out=keysT[:, hp - 3:hp + 1, :], in_=psK[:]), ['psK'], ['keysT'])
            load_w_bf(T, wq, 'wq', lambda k: peer_wq[l, k * 128:(k + 1) * 128, :], 2048, stg, 8)

            order = [(0, i) for i in range(16)] + [(1, i) for i in range(16)]
            if not last:
                order += [(0, 16), (0, 17), (1, 16), (1, 17)]
            order = order[:KPT]
            cur = {'r': -1, 'set': 1}
            tset = {}
            ctr = {'u': 0, 'v': 0}
            S.dma('sp', xt[0][:], xr[order[0][0], order[0][1] * 128:(order[0][1] + 1) * 128, :], writes=[('xt', 0)])

            def prologue(n):
                s, i = order[n]
                j = n % 2
                r = s if i < 16 else 2
                if r != cur['r']:
                    cur['set'] = 1 - cur['set']
                    for jj, col in enumerate((3, 4, 5)):
                        S.dma('sp', mbs[cur['set']][jj][:], modv_d[l, r, col:col + 1, :].broadcast_to([128, D]),
                              writes=[('mb', cur['set'])])
                    cur['r'] = r
                tset[n] = cur['set']
                mb = mbs[tset[n]]
                mbk = ('mb', tset[n])
                X = xt[j]
                S.op('act', lambda g: g.activation(out=junk[:], in_=X[:], func=AF.Square), [('xt', j)], ['junk'])
                S.op('dve', lambda g: g.tensor_reduce(out=ss[:, 0:1], in_=junk[:], axis=AX.X, op=ALU.add), ['junk'], ['ss'])
                S.op('dve', lambda g: g.tensor_scalar(out=ss[:, 1:2], in0=ss[:, 0:1], scalar1=1.0 / D, scalar2=EPS,
                                                      op0=ALU.mult, op1=ALU.add), ['ss'], ['ss'])
                S.op('act', lambda g: g.activation(out=ss[:, 2:3], in_=ss[:, 1:2], func=AF.Sqrt), ['ss'], ['ss'])
                S.op('dve', lambda g: g.reciprocal(out=ss[:, 3:4], in_=ss[:, 2:3]), ['ss'], ['ss'])
                S.op('dve', lambda g: g.scalar_tensor_tensor(out=junk[:], in0=X[:], scalar=ss[:, 3:4], op0=ALU.mult,
                                                             in1=mb[0][:], op1=ALU.mult), [('xt', j), 'ss', mbk, 'junk'], ['junk'])
                S.op('dve', lambda g: g.tensor_tensor(out=h2[:], in0=junk[:], in1=mb[1][:], op=ALU.add), ['junk', mbk], ['h2'])
                S.op('act', lambda g: g.copy(out=h2b[:], in_=h2[:]), ['h2'], ['h2b'])
                for k in range(8):
                    S.op('pe', lambda g: g.transpose(out=psT[:, k, :], in_=h2b[:, k * 128:(k + 1) * 128], identity=identb[:]),
                         ['h2b', 'identb'], ['psT'])
                S.op('act', lambda g: g.copy(out=h2T[:], in_=psT[:]), ['psT'], ['h2T'])
                for hp in range(16):
                    for k in range(8):
                        S.op('pe', lambda g: g.matmul(psQ[:, hp, :], lhsT=wq[:, k, hp * 128:(hp + 1) * 128], rhs=h2T[:, k, :],
                                                      start=(k == 0), stop=(k == 7)), ['wq', 'h2T'], [('psQ', hp // 8)])
                S.op('dve', lambda g: g.tensor_copy(out=qTs[:, 0:8, :], in_=psQ[:, 0:8, :]), [('psQ', 0)], ['qTs'])
                S.op('act', lambda g: g.copy(out=qTs[:, 8:16, :], in_=psQ[:, 8:16, :]), [('psQ', 1)], ['qTs'])
                for hp in range(16):
                    S.op('pe', lambda g: g.matmul(psQ[:, hp, :], lhsT=qTs[:, hp, :], rhs=keysT[:, hp, :], start=True, stop=True),
                         ['qTs', 'keysT'], [('psQ', hp // 8)])
                S.op('dve', lambda g: g.tensor_copy(out=sc[:, 0:8, :], in_=psQ[:, 0:8, :]), [('psQ', 0)], ['sc'])
                S.op('act', lambda g: g.copy(out=sc[:, 8:16, :], in_=psQ[:, 8:16, :]), [('psQ', 1)], ['sc'])
                for hp in range(16):
                    S.op('dve', lambda g: g.max(out=V[:, hp, 0:8], in_=sc[:, hp, :]), ['sc'], ['V'])
                    S.op('dve', lambda g: g.max_index(out=I[:, hp, 0:8], in_max=V[:, hp, 0:8], in_values=sc[:, hp, :]),
                         ['sc', 'V'], ['I'])
                    S.op('dve', lambda g: g.match_replace(out=wk[:, 0:128], in_to_replace=V[:, hp, 0:8], in_values=sc[:, hp, :],
                                                          imm_value=NEG), ['sc', 'V'], ['wk'])
                    S.op('dve', lambda g: g.max(out=V[:, hp, 8:16], in_=wk[:, 0:128]), ['wk'], ['V'])
                    S.op('dve', lambda g: g.max_index(out=I[:, hp, 8:16], in_max=V[:, hp, 8:16], in_values=wk[:, 0:128]),
                         ['wk', 'V'], ['I'])
                S.op('dve', lambda g: g.tensor_copy(out=If[:], in_=I[:]), ['I'], ['If'])
                V4 = V[:].rearrange("p (h t) k -> p h t k", t=2)
                I4 = If[:].rearrange("p (h t) k -> p h t k", t=2)
                S.op('dve', lambda g: g.tensor_tensor(
                    out=cand[:].rearrange("p h (a b) -> p h a b", b=16),
                    in0=V4[:, :, 0, :].unsqueeze(3).broadcast_to([128, 8, 16, 16]),
                    in1=V4[:, :, 1, :].unsqueeze(2).broadcast_to([128, 8, 16, 16]), op=ALU.add), ['V'], ['cand'])
                for h in range(8):
                    S.op('dve', lambda g: g.max(out=Tt[:, h, 0:8], in_=cand[:, h, :]), ['cand'], ['Tt'])
                    S.op('dve', lambda g: g.max_index(out=Pp[:, h, 0:8], in_max=Tt[:, h, 0:8], in_values=cand[:, h, :]),
                         ['cand', 'Tt'], ['Pp'])
                    S.op('dve', lambda g: g.match_replace(out=wk[:], in_to_replace=Tt[:, h, 0:8], in_values=cand[:, h, :],
                                                          imm_value=NEG), ['cand', 'Tt'], ['wk'])
                    S.op('dve', lambda g: g.max(out=Tt[:, h, 8:16], in_=wk[:]), ['wk'], ['Tt'])
                    S.op('dve', lambda g: g.max_index(out=Pp[:, h, 8:16], in_max=Tt[:, h, 8:16], in_values=wk[:]),
                         ['wk', 'Tt'], ['Pp'])
                S.op('dve', lambda g: g.tensor_single_scalar(out=pa[:], in_=Pp[:], scalar=4, op=ALU.logical_shift_right),
                     ['Pp'], ['pa'])
                S.op('dve', lambda g: g.tensor_single_scalar(out=pbb[:], in_=Pp[:], scalar=15, op=ALU.bitwise_and),
                     ['Pp'], ['pbb'])
                S.op('dve', lambda g: g.tensor_copy(out=paf[:], in_=pa[:]), ['pa'], ['paf'])
                S.op('dve', lambda g: g.tensor_copy(out=pbf[:], in_=pbb[:]), ['pbb'], ['pbf'])
                iob = iota[:].unsqueeze(1).unsqueeze(1).broadcast_to([128, 8, 16, 16])
                for (pf, pk, tsel, outi, ok) in ((paf, 'paf', 0, ix1, 'i1'), (pbf, 'pbf', 1, ix2, 'i2')):
                    S.op('dve', lambda g: g.tensor_tensor(out=eq[:], in0=pf[:].unsqueeze(3).broadcast_to([128, 8, 16, 16]),
                                                          in1=iob, op=ALU.is_equal), [pk, 'iota', 'eq'], ['eq'])
                    S.op('dve', lambda g: g.tensor_tensor(out=eq[:], in0=eq[:],
                                                          in1=I4[:, :, tsel, :].unsqueeze(2).broadcast_to([128, 8, 16, 16]),
                                                          op=ALU.mult), ['eq', 'If'], ['eq'])
                    S.op('dve', lambda g: g.tensor_reduce(out=outi[:], in_=eq[:], axis=AX.X, op=ALU.add), ['eq'], [ok])
                E = eidx[j]
                S.op('dve', lambda g: g.scalar_tensor_tensor(out=ef[:].rearrange("p (h k) -> p h k", k=16), in0=ix1[:], scalar=128.0,
                                                             op0=ALU.mult, in1=ix2[:], op1=ALU.add), ['i1', 'i2'], ['ef'])
                S.op('dve', lambda g: g.tensor_copy(out=E[:], in_=ef[:]), ['ef'], [('eidx', j)])
                if eidx_d is not None:
                    S.dma('act', eidx_d[n], E[:], reads=[('eidx', j)])
                S.op('dve', lambda g: g.tensor_tensor(out=dd[:], in0=Tt[:], in1=Tt[:, :, 0:1].broadcast_to([128, 8, 16]),
                                                      op=ALU.subtract), ['Tt'], ['dd'])
                S.op('act', lambda g: g.activation(out=dd[:], in_=dd[:], func=AF.Exp), ['dd'], ['dd'])
                S.op('dve', lambda g: g.tensor_reduce(out=zz[:], in_=dd[:], axis=AX.X, op=ALU.add), ['dd'], ['zz'])
                S.op('dve', lambda g: g.reciprocal(out=zz[:], in_=zz[:]), ['zz'], ['zz'])
                S.op('dve', lambda g: g.tensor_tensor(out=gg[:].rearrange("p (h k) -> p h k", k=16), in0=dd[:],
                                                      in1=zz[:].unsqueeze(2).broadcast_to([128, 8, 16]), op=ALU.mult),
                     ['dd', 'zz'], ['gg'])

            def uphase(n):
                s, i = order[n]
                j = n % 2
                E = eidx[j]
                ngu = ctr['u']
                for jj in range(128):
                    b = ngu % NB
                    ngu += 1
                    S.gather(Ug[b][:], ub_d, E[:, jj:jj + 1], reads=[('eidx', j)], writes=[('Ug', b)])
                    S.op('dve', lambda g: g.scalar_tensor_tensor(out=uj[:], in0=Ug[b][:], scalar=1.0, op0=ALU.mult,
                                                                 in1=h2b[:], op1=ALU.mult, accum_out=araw[:, jj:jj + 1]),
                         [('Ug', b), 'h2b'], ['uj', 'araw'])
                S.op('dve', lambda g: g.tensor_tensor(out=g1[:], in0=araw[:], in1=araw[:], op=ALU.mult), ['araw'], ['g1'])
                S.op('dve', lambda g: g.tensor_scalar(out=g1[:], in0=g1[:], scalar1=0.044715, scalar2=1.0, op0=ALU.mult,
                                                      op1=ALU.add), ['g1'], ['g1'])
                S.op('dve', lambda g: g.tensor_tensor(out=g1[:], in0=g1[:], in1=araw[:], op=ALU.mult), ['g1', 'araw'], ['g1'])
                S.op('act', lambda g: g.activation(out=g2[:], in_=g1[:], func=AF.Tanh, scale=0.7978845608028654), ['g1'], ['g2'])
                S.op('dve', lambda g: g.tensor_scalar(out=g2[:], in0=g2[:], scalar1=1.0, scalar2=0.5, op0=ALU.add,
                                                      op1=ALU.mult), ['g2'], ['g2'])
                S.op('dve', lambda g: g.tensor_tensor(out=g2[:], in0=g2[:], in1=araw[:], op=ALU.mult), ['g2', 'araw'], ['g2'])
                S.op('dve', lambda g: g.tensor_tensor(out=wgt[:], in0=g2[:], in1=gg[:], op=ALU.mult), ['g2', 'gg'], ['wgt'])
                ctr['u'] = ngu

            def vphase(n):
                s, i = order[n]
                j = n % 2
                E = eidx[j]
                ngv = ctr['v']
                for jj in range(128):
                    b = ngv % NB
                    ngv += 1
                    S.gather(Vg[b][:], vb_d, E[:, jj:jj + 1], reads=[('eidx', j)], writes=[('Vg', b)])
                    S.op('act', lambda g: g.activation(out=dg[b][:], in_=identb[:], func=AF.Copy, scale=wgt[:, jj:jj + 1]),
                         ['identb', 'wgt'], [('dg', b)])
                    for half in range(2):
                        S.op('pe', lambda g: g.matmul(psV[:, half * 512:(half + 1) * 512], lhsT=dg[b][:],
                                                      rhs=Vg[b][:, half * 512:(half + 1) * 512],
                                                      start=(jj == 0), stop=(jj == 127)),
                             [('dg', b), ('Vg', b)], [('psV', half)])
                ctr['v'] = ngv

            def epilogue(n):
                s, i = order[n]
                j = n % 2
                X = xt[j]
                mb = mbs[tset[n]]
                mbk = ('mb', tset[n])
                S.op('dve', lambda g: g.tensor_tensor(out=acc[:], in0=psV[:], in1=mb[2][:], op=ALU.mult),
                     [('psV', 0), ('psV', 1), mbk], ['acc'])
                S.op('dve', lambda g: g.tensor_tensor(out=xn[j][:], in0=acc[:], in1=X[:], op=ALU.add),
                     ['acc', ('xt', j)], [('xn', j)])
                if last:
                    dst = y[s, i * 128:(i + 1) * 128, :]
                else:
                    dst = (xo if peer_only else xr)[s, i * 128:(i + 1) * 128, :]
                S.dma('act', dst, xn[j][:], reads=[('xn', j)])

            prologue(0)
            for n in range(len(order)):
                j = n % 2
                if n + 1 < len(order):
                    s2, i2 = order[n + 1]
                    S.dma('sp', xt[1 - j][:], xr[s2, i2 * 128:(i2 + 1) * 128, :], writes=[('xt', 1 - j)])
                uphase(n)
                if n + 1 < len(order):
                    prologue(n + 1)
                vphase(n)
                epilogue(n)


        phases = []
        for l in range(depth):
            last = (l == depth - 1) and not all_ctx
            phases += [(ph_mod, l), (ph_m1, l), (ph_f, l, last), (ph_p, l, last), (ph_a, l, last), (ph_m3, l, last),
                       (ph_conv, l), (ph_peer, l, last)]
        if stop_after is not None:
            phases = phases[:stop_after]
        if peer_only:
            phases = [(ph_conv, 0), (ph_peer, 0, False)]
        for p in phases:
            run_phase(p[0], *p[1:])
    return nc, S


def host_consts():
    bf = ml_dtypes.bfloat16
    t = np.arange(L, dtype=np.float64)
    ang = 2 * np.pi * ((np.outer(t, t)) % L) / L
    dftc = np.cos(ang).astype(bf)
    dfts = np.sin(ang).astype(bf)
    t2 = np.arange(LC, dtype=np.float64)
    ang2 = 2 * np.pi * ((np.outer(t2, t2)) % LC) / LC
    dftc2 = np.cos(ang2).astype(bf)
    dfts2 = np.sin(ang2).astype(bf)
    c = np.arange(64, dtype=np.float64)
    a64 = 2 * np.pi * ((np.outer(c, c)) % 64) / 64
    chd = np.zeros((128, 256), np.float64)
    for g in range(2):
        chd[g * 64:(g + 1) * 64, g * 64:(g + 1) * 64] = np.cos(a64)
        chd[g * 64:(g + 1) * 64, 128 + g * 64:128 + (g + 1) * 64] = -np.sin(a64)
    chd = chd.astype(bf)
    pos = np.arange(L)
    row = (pos // 64).astype(np.float32)
    col = (pos % 64).astype(np.float32)
    inv = (np.float32(10000.0) ** (-np.arange(16, dtype=np.float32) / np.float32(16))).astype(np.float32)
    ar = (row[:, None] * inv[None, :]).astype(np.float32)
    ac = (col[:, None] * inv[None, :]).astype(np.float32)
    ropec = np.concatenate([np.cos(ar), np.cos(ac)], axis=1).astype(np.float32)
    ropes = np.concatenate([np.sin(ar), np.sin(ac)], axis=1).astype(np.float32)

    def invcnt(n):
        out = np.zeros((2, 128, n), np.float32)
        tt = np.arange(n)
        for gi, win in enumerate((2, 4, 8, 16)):
            lo = np.maximum(tt - win // 2, 0)
            hi = np.minimum(tt + win - win // 2, n)
            out[gi // 2, (gi % 2) * 64:(gi % 2 + 1) * 64, :] = (1.0 / (hi - lo).astype(np.float32))[None, :]
        return out
    return {
        "dftc": dftc, "dfts": dfts, "dftc2": dftc2, "dfts2": dfts2, "chd": chd,
        "ropec": ropec, "ropes": ropes, "invl": invcnt(L), "invx": invcnt(LC),
        "identb": np.eye(128).astype(bf), "identf": np.eye(128, dtype=np.float32),
        "iota16": np.tile(np.arange(16, dtype=np.float32)[None, :], (128, 1)),
    }


WNAMES = ["ada_w", "ada_b", "norm_mix", "w_in", "fourier_w", "pool_w", "pool_scale", "pool_proj", "q_norm", "k_norm",
          "attn_proj", "w_out", "norm_ffn", "peer_wq", "peer_keys", "peer_u", "peer_v"]


def make_in_maps(inputs, ncores=8):
    consts = host_consts()
    w = {k: np.ascontiguousarray(np.asarray(inputs[k], dtype=np.float32)) for k in WNAMES}
    x = np.asarray(inputs["x"], dtype=np.float32)
    c = np.asarray(inputs["c"], dtype=np.float32)
    ctx = np.asarray(inputs["ctx"], dtype=np.float32)
    c_ctx = np.asarray(inputs["c_ctx"], dtype=np.float32)
    maps = []
    for core in range(ncores):
        b0 = 2 * core
        cc = np.stack([c[b0], c[b0 + 1], c_ctx], axis=0)
        cct = np.ascontiguousarray(cc.reshape(3, 8, 128).transpose(2, 1, 0).reshape(128, 24))
        m = {"x": np.ascontiguousarray(x[b0:b0 + 2]), "ctx": np.ascontiguousarray(ctx[b0:b0 + 2]), "cct": cct}
        m.update(w)
        m.update(consts)
        maps.append(m)
    return maps


def kernel(**inputs):
    nc, _ = build(depth=1, all_ctx=True, WD=1, xr_out=True)
    consts = host_consts()
    x = np.asarray(inputs["x"], dtype=np.float32)
    ctx = np.asarray(inputs["ctx"], dtype=np.float32)
    c = np.asarray(inputs["c"], dtype=np.float32)
    c_ctx = np.asarray(inputs["c_ctx"], dtype=np.float32)
    ccts = []
    for core in range(8):
        cc = np.stack([c[2 * core], c[2 * core + 1], c_ctx], axis=0)
        ccts.append(np.ascontiguousarray(cc.reshape(3, 8, 128).transpose(2, 1, 0).reshape(128, 24)))
    xs = [np.ascontiguousarray(x[2 * i:2 * i + 2]) for i in range(8)]
    cs = [np.ascontiguousarray(ctx[2 * i:2 * i + 2]) for i in range(8)]
    for l in range(DEPTH):
        w = {k: np.ascontiguousarray(np.asarray(inputs[k], dtype=np.float32)[l:l + 1]) for k in WNAMES}
        maps = []
        for core in range(8):
            m = {"x": xs[core], "ctx": cs[core], "cct": ccts[core]}
            m.update(w)
            m.update(consts)
            maps.append(m)
        res = run_bass_kernel_spmd(nc, maps, core_ids=list(range(8)))
        outs = [np.asarray(r["xr"]) for r in res.results]
        xs = [np.ascontiguousarray(o[:, :L]) for o in outs]
        cs = [np.ascontiguousarray(o[:, L:]) for o in outs]
    return np.concatenate(xs, axis=0).astype(np.float32)
```

```python
import numpy as np
import ml_dtypes
from contextlib import ExitStack
import concourse.bass as bass
import concourse.mybir as mybir
from concourse.bass_utils import run_bass_kernel_spmd

F32 = mybir.dt.float32
BF = mybir.dt.bfloat16
U32 = mybir.dt.uint32
AF = mybir.ActivationFunctionType
ALU = mybir.AluOpType
AX = mybir.AxisListType

D = 1024
L = 2048
LC = 256
LT = L + LC
NT = 18
DEPTH = 4
EPS = 1e-6
NEXP = 16384
NEG = -1.0e30
KCUT = 99
KTILES = 99
import os
KPT = int(os.environ.get('KPT', '99'))


class Sch:
    NDS = 24
    NHW = 16

    def __init__(s, nc, es):
        s.nc = nc
        s.E = {'pe': nc.tensor, 'act': nc.scalar, 'dve': nc.vector, 'pool': nc.gpsimd, 'sp': nc.sync}
        s.sem = {k: es.enter_context(nc.semaphore('sem_' + k)) for k in ('pe', 'act', 'dve', 'pool')}
        s.cnt = {k: 0 for k in s.sem}
        s.dsem = [es.enter_context(nc.semaphore('dsem%d' % i)) for i in range(s.NDS)]
        s.dcnt = [0] * s.NDS
        s.dn = 0
        s.dnsw = 0
        s.waited = {}
        s.lw = {}
        s.rd = {}
        s.ninst = 0

    def _semobj(s, n):
        return s.sem[n] if isinstance(n, str) else s.dsem[n]

    def _wait(s, eng, ev):
        n, v = ev
        if s.waited.get((eng, n), 0) >= v:
            return
        s.E[eng].wait_ge(s._semobj(n), v)
        s.waited[(eng, n)] = v

    def _deps(s, eng, reads, writes):
        evs = []
        for k in reads:
            if k in s.lw:
                evs.append(s.lw[k])
        for k in writes:
            if k in s.lw:
                evs.append(s.lw[k])
            evs.extend(s.rd.get(k, {}).items())
        for ev in evs:
            if eng == 'pe' and ev[0] == 'pe':
                continue
            s._wait(eng, ev)

    def _record(s, ev, reads, writes):
        for k in reads:
            d = s.rd.setdefault(k, {})
            d[ev[0]] = max(d.get(ev[0], 0), ev[1])
        for k in writes:
            s.lw[k] = ev
            s.rd[k] = {}

    @staticmethod
    def _isps(k):
        n = k if isinstance(k, str) else k[0]
        return isinstance(n, str) and n.startswith('ps')

    def op(s, eng, fn, reads=(), writes=()):
        writes = list(writes) + [k for k in reads if s._isps(k)]
        reads = [k for k in reads if not s._isps(k)]
        s._deps(eng, reads, writes)
        ins = fn(s.E[eng])
        s.cnt[eng] += 1
        ins.then_inc(s.sem[eng], 1)
        ev = (eng, s.cnt[eng])
        s._record(ev, reads, writes)
        s.ninst += 1
        return ev

    def _pick(s, q):
        if q == 'pool':
            i = s.NHW + s.dnsw
            s.dnsw = (s.dnsw + 1) % (s.NDS - s.NHW)
        else:
            i = s.dn
            s.dn = (s.dn + 1) % s.NHW
        return i

    def dma(s, q, out, in_, reads=(), writes=(), **kw):
        i = s._pick(q)
        if s.dcnt[i]:
            s._wait(q, (i, s.dcnt[i]))
        s._deps(q, reads, writes)
        ins = s.E[q].dma_start(out=out, in_=in_, **kw)
        ins.then_inc(s.dsem[i], 16)
        s.dcnt[i] += 16
        ev = (i, s.dcnt[i])
        s._record(ev, reads, writes)
        s.ninst += 1
        return ev

    def gather(s, out, table, idx, reads=(), writes=()):
        q = 'pool'
        i = s._pick(q)
        if s.dcnt[i]:
            s._wait(q, (i, s.dcnt[i]))
        s._deps(q, reads, writes)
        ins = s.E[q].indirect_dma_start(out=out, out_offset=None, in_=table,
                                        in_offset=bass.IndirectOffsetOnAxis(ap=idx, axis=0))
        ins.then_inc(s.dsem[i], 16)
        s.dcnt[i] += 16
        ev = (i, s.dcnt[i])
        s._record(ev, reads, writes)
        s.ninst += 1
        return ev

    def barrier(s):
        evs = [(k, s.cnt[k]) for k in s.sem if s.cnt[k]] + [(i, c) for i, c in enumerate(s.dcnt) if c]
        for eng in s.E:
            for ev in evs:
                s._wait(eng, ev)
        s.lw = {}
        s.rd = {}


def build(depth=DEPTH, dbg=False, stop_after=None, all_ctx=False, WD=DEPTH, xr_out=False, peer_only=False):
    nc = bass.Bass("TRN2", target_bir_lowering=False)

    def din(name, shape, dt=F32):
        return nc.dram_tensor(name, list(shape), dt, kind="ExternalInput").ap()

    def dsc(name, shape, dt=F32):
        return nc.dram_tensor(name, list(shape), dt, kind=("ExternalOutput" if dbg else "Internal")).ap()

    x_in = din("x", [2, L, D])
    ctx_in = din("ctx", [2, LC, D])
    cct = din("cct", [128, 24])
    ada_w = din("ada_w", [WD, D, 6 * D])
    ada_b = din("ada_b", [WD, 6 * D])
    norm_mix = din("norm_mix", [WD, D])
    w_in = din("w_in", [WD, D, 4352])
    fourier_w = din("fourier_w", [WD, 256, D])
    pool_w = din("pool_w", [WD, 4, 64, 64])
    pool_scale = din("pool_scale", [WD, 256])
    pool_proj = din("pool_proj", [WD, 256, D])
    q_norm = din("q_norm", [WD, 64])
    k_norm = din("k_norm", [WD, 64])
    attn_proj = din("attn_proj", [WD, 512, D])
    w_out = din("w_out", [WD, D, D])
    norm_ffn = din("norm_ffn", [WD, D])
    peer_wq = din("peer_wq", [WD, D, 2048])
    peer_keys = din("peer_keys", [WD, 8, 2, 128, 128])
    need_peer = stop_after is None or stop_after >= 7 or peer_only
    peer_u = din("peer_u", [WD, NEXP, D]) if need_peer else None
    peer_v = din("peer_v", [WD, NEXP, D]) if need_peer else None
    dftc = din("dftc", [L, L], BF)
    dfts = din("dfts", [L, L], BF)
    dftc2 = din("dftc2", [LC, LC], BF)
    dfts2 = din("dfts2", [LC, LC], BF)
    chd_d = din("chd", [128, 256], BF)
    ropec = din("ropec", [L, 32])
    ropes = din("ropes", [L, 32])
    invl_d = din("invl", [2, 128, L])
    invx_d = din("invx", [2, 128, LC])
    identb_d = din("identb", [128, 128], BF)
    identf_d = din("identf", [128, 128])
    iota_d = din("iota16", [128, 16])

    y = None if xr_out else nc.dram_tensor("y", [2, L, D], F32, kind="ExternalOutput").ap()

    xr = nc.dram_tensor("xr", [2, LT, D], F32, kind="ExternalOutput").ap() if xr_out else (din("xr_in", [2, LT, D]) if peer_only else dsc("xr", [2, LT, D]))
    xo = nc.dram_tensor("xo", [2, LT, D], F32, kind="ExternalOutput").ap() if peer_only else None
    hT_d = dsc("hT_d", [2, 8, 128, LT], BF)
    XCS_d = dsc("XCS_d", [2, NT, 128, 512], BF)
    pT_d = dsc("pT_d", [2, 2, 128, LT])
    qT_d = dsc("qT_d", [2, 64, 8, LT], BF)
    kT_d = dsc("kT_d", [2, 64, 2, LT], BF)
    v1_d = dsc("v1_d", [2, 128, NT, 130], BF)
    ZT_d = dsc("ZT_d", [2, 2, 128, LT], BF)
    plT_d = dsc("plT_d", [2, 2, 128, LT], BF)
    yaT_d = dsc("yaT_d", [2, 4, 128, LT], BF)
    modv_d = din("modv_in", [WD, 3, 6, D]) if peer_only else dsc("modv_d", [WD, 3, 6, D])
    eidx_d = dsc("eidx_d", [36, 128, 128], U32) if dbg else None
    ub_d = nc.dram_tensor("ub_d", [NEXP, D], BF, kind="Internal").ap()
    vb_d = nc.dram_tensor("vb_d", [NEXP, D], BF, kind="Internal").ap()

    def xsrc(l, s, i):
        if l == 0:
            if i < 16:
                return x_in[s, i * 128:(i + 1) * 128, :]
            return ctx_in[s, (i - 16) * 128:(i - 15) * 128, :]
        return xr[s, i * 128:(i + 1) * 128, :]

    with ExitStack() as es:
        S = Sch(nc, es)

        def run_phase(fn, *a):
            with ExitStack() as ph:
                cnt = [0]

                def T(shape, dt=F32, name=None):
                    cnt[0] += 1
                    return ph.enter_context(nc.sbuf_tensor(name or ("t%d_%d" % (S.ninst, cnt[0])), list(shape), dt))

                def P(shape, dt=F32, name=None):
                    cnt[0] += 1
                    return ph.enter_context(nc.psum_tensor(name or ("p%d_%d" % (S.ninst, cnt[0])), list(shape), dt))

                fn(T, P, *a)
                S.barrier()

        castrr = [0]

        def cast(out, in_, reads, writes):
            e = ('dve', 'act')[castrr[0] % 2]
            castrr[0] += 1
            if e == 'act':
                S.op('act', lambda g: g.copy(out=out, in_=in_), reads, writes)
            else:
                S.op(e, lambda g: g.tensor_copy(out=out, in_=in_), reads, writes)

        def load_w_bf(T, dst, dkey, src_rows, ncols, stg, nk):
            cw = stg[0].shape[-1]
            n = 0
            for k in range(nk):
                for c0 in range(0, ncols, cw):
                    c1 = min(ncols, c0 + cw)
                    j = n % 2
                    n += 1
                    S.dma('sp', stg[j][:, 0:c1 - c0], src_rows(k)[:, c0:c1], writes=[('stg', id(stg), j)])
                    cast(dst[:, k, c0:c1], stg[j][:, 0:c1 - c0], [('stg', id(stg), j)], [dkey])

        def ph_mod(T, P, l):
            cc = T([128, 24])
            sc = T([128, 24])
            adab = T([3, 6 * D])
            mods = T([3, 6 * D])
            gm = T([3, D])
            gf = T([3, D])
            mo = T([3, 6, D])
            wst = [T([128, 8, 512]) for _ in range(2)]
            ps = [P([128, 512]) for _ in range(2)]
            S.dma('sp', cc[:], cct, writes=['cc'])
            S.dma('sp', adab[:], ada_b[l:l + 1, :].broadcast_to([3, 6 * D]), writes=['adab'])
            S.dma('sp', gm[:], norm_mix[l:l + 1, :].broadcast_to([3, D]), writes=['gm'])
            S.dma('sp', gf[:], norm_ffn[l:l + 1, :].broadcast_to([3, D]), writes=['gf'])
            S.op('act', lambda g: g.activation(out=sc[:], in_=cc[:], func=AF.Silu), ['cc'], ['sc'])
            for nb in range(12):
                j = nb % 2
                for k2 in range(2):
                    S.dma('sp', wst[j][:, k2 * 4:(k2 + 1) * 4, :],
                          ada_w[l, k2 * 512:(k2 + 1) * 512, nb * 512:(nb + 1) * 512].rearrange("(k p) n -> p k n", p=128),
                          writes=[('wst', j)])
                for k in range(8):
                    S.op('pe', lambda g: g.matmul(ps[j][0:3, :], lhsT=sc[:, k * 3:(k + 1) * 3], rhs=wst[j][:, k, :],
                                                  start=(k == 0), stop=(k == 7)),
                         ['sc', ('wst', j)], [('psm', j)])
                S.op('dve', lambda g: g.tensor_tensor(out=mods[:, nb * 512:(nb + 1) * 512], in0=ps[j][0:3, :],
                                                      in1=adab[:, nb * 512:(nb + 1) * 512], op=ALU.add),
                     [('psm', j), 'adab'], ['mods'])
            S.op('dve', lambda g: g.scalar_tensor_tensor(out=mo[:, 0, :], in0=mods[:, D:2 * D], scalar=1.0, op0=ALU.add,
                                                         in1=gm[:], op1=ALU.mult), ['mods', 'gm'], ['mo'])
            S.op('dve', lambda g: g.scalar_tensor_tensor(out=mo[:, 3, :], in0=mods[:, 4 * D:5 * D], scalar=1.0, op0=ALU.add,
                                                         in1=gf[:], op1=ALU.mult), ['mods', 'gf'], ['mo'])
            for dst, src in ((1, 0), (2, 2), (4, 3), (5, 5)):
                S.op('dve', lambda g: g.tensor_copy(out=mo[:, dst, :], in_=mods[:, src * D:(src + 1) * D]), ['mods'], ['mo'])
            S.dma('sp', modv_d[l], mo[:], reads=['mo'])

        def ph_m1(T, P, l):
            W1 = T([128, 8, 1280], BF)
            stg = [T([128, 1280]) for _ in range(2)]
            identb = T([128, 128], BF)
            chd = T([128, 256], BF)
            gq = T([128, 640])
            rc = T([128, 16, 32])
            rs = T([128, 16, 32])
            modb = [[T([128, D]) for _ in range(2)] for _ in range(3)]
            xt = [T([128, D]) for _ in range(2)]
            junk = T([128, D])
            tmp = T([128, D])
            hb = T([128, D], BF)
            hT = T([128, 8, 128], BF)
            fT = T([128, 2, 128], BF)
            xcs = T([128, 512], BF)
            pt = T([128, 2, 128])
            sqq = T([128, 640])
            qn = T([128, 640])
            ta = T([128, 320])
            tb = T([128, 320])
            tcx = T([128, 320])
            td = T([128, 320])
            qr = T([128, 640], BF)
            qTs = T([64, 8, 128], BF)
            kTs = T([64, 2, 128], BF)
            v1 = [T([128, 130], BF) for _ in range(2)]
            ss = T([128, 4])
            ssq = T([128, 10])
            rq = T([128, 10])
            psq = P([128, 1024])
            psT = P([128, 8, 128], BF)
            psfp = P([128, 4, 128])
            psx = P([128, 512])
            psqT = P([64, 8, 128], BF)
            pskT = P([64, 8, 128], BF)
            psv = P([128, 512])

            S.dma('sp', identb[:], identb_d, writes=['identb'])
            S.dma('sp', chd[:], chd_d, writes=['chd'])
            S.dma('sp', gq[:, 0:512].rearrange("p (h d) -> p h d", d=64),
                  bass.AP(tensor=q_norm.tensor, offset=l * 64, ap=[[0, 128], [0, 8], [1, 64]]),
                  writes=['gq'])
            S.dma('sp', gq[:, 512:640].rearrange("p (h d) -> p h d", d=64),
                  bass.AP(tensor=k_norm.tensor, offset=l * 64, ap=[[0, 128], [0, 2], [1, 64]]),
                  writes=['gq'])
            for q4 in range(4):
                S.dma('sp', rc[:, q4 * 4:(q4 + 1) * 4, :], ropec[q4 * 512:(q4 + 1) * 512, :].rearrange("(t p) c -> p t c", p=128),
                      writes=['rope'])
                S.dma('sp', rs[:, q4 * 4:(q4 + 1) * 4, :], ropes[q4 * 512:(q4 + 1) * 512, :].rearrange("(t p) c -> p t c", p=128),
                      writes=['rope'])
            for r in range(3):
                for j in range(2):
                    S.dma('sp', modb[r][j][:], modv_d[l, r, j:j + 1, :].broadcast_to([128, D]), writes=['modb'])
            for j in range(2):
                S.op('dve', lambda g: g.memset(v1[j][:], 1.0), (), [('v1', j)])
            load_w_bf(T, W1, 'W1', lambda k: w_in[l, k * 128:(k + 1) * 128, 0:1280], 1280, stg, 8)

            tiles = [(s, i) for s in range(2) for i in range(NT)][:KTILES]
            S.dma('sp', xt[0][:], xsrc(l, 0, 0), writes=[('xt', 0)])
            for n, (s, i) in enumerate(tiles):
                j = n % 2
                if n + 1 < len(tiles):
                    s2, i2 = tiles[n + 1]
                    S.dma('sp', xt[1 - j][:], xsrc(l, s2, i2), writes=[('xt', 1 - j)])
                r = s if i < 16 else 2
                t0 = i * 128
                X = xt[j]
                S.op('act', lambda g: g.activation(out=junk[:], in_=X[:], func=AF.Square), [('xt', j)], ['junk'])
                S.op('dve', lambda g: g.tensor_reduce(out=ss[:, 0:1], in_=junk[:], axis=AX.X, op=ALU.add), ['junk'], ['ss'])
                S.op('dve', lambda g: g.tensor_scalar(out=ss[:, 1:2], in0=ss[:, 0:1], scalar1=1.0 / D, scalar2=EPS,
                                                      op0=ALU.mult, op1=ALU.add), ['ss'], ['ss'])
                S.op('act', lambda g: g.activation(out=ss[:, 2:3], in_=ss[:, 1:2], func=AF.Sqrt), ['ss'], ['ss'])
                S.op('dve', lambda g: g.reciprocal(out=ss[:, 3:4], in_=ss[:, 2:3]), ['ss'], ['ss'])
                S.op('dve', lambda g: g.scalar_tensor_tensor(out=tmp[:], in0=X[:], scalar=ss[:, 3:4], op0=ALU.mult,
                                                             in1=modb[r][0][:], op1=ALU.mult),
                     [('xt', j), 'ss', 'modb'], ['tmp'])
                S.op('dve', lambda g: g.tensor_tensor(out=hb[:], in0=tmp[:], in1=modb[r][1][:], op=ALU.add),
                     ['tmp', 'modb'], ['hb'])
                for k in range(8):
                    S.op('pe', lambda g: g.transpose(out=psT[:, k, :], in_=hb[:, k * 128:(k + 1) * 128], identity=identb[:]),
                         ['hb', 'identb'], ['psT'])
                S.op('act', lambda g: g.copy(out=hT[:], in_=psT[:]), ['psT'], ['hT'])
                for k2 in range(2):
                    S.dma('act', hT_d[s, k2 * 4:(k2 + 1) * 4, :, t0:t0 + 128].rearrange("k p t -> p k t"),
                          hT[:, k2 * 4:(k2 + 1) * 4, :], reads=['hT'])
                if KCUT <= 1:
                    continue
                for c in range(4):
                    for k in range(8):
                        S.op('pe', lambda g: g.matmul(psfp[:, c, :], lhsT=W1[:, k, c * 128:(c + 1) * 128], rhs=hT[:, k, :],
                                                      start=(k == 0), stop=(k == 7)), ['W1', 'hT'], ['psfp'])
                S.op('dve', lambda g: g.tensor_copy(out=fT[:], in_=psfp[:, 0:2, :]), ['psfp'], ['fT'])
                S.op('act', lambda g: g.copy(out=pt[:], in_=psfp[:, 2:4, :]), ['psfp'], ['pt'])
                S.dma('act', pT_d[s, :, :, t0:t0 + 128].rearrange("c p t -> p c t"), pt[:], reads=['pt'])
                if KCUT <= 2:
                    continue
                for c in range(2):
                    S.op('pe', lambda g: g.matmul(psx[:, c * 256:(c + 1) * 256], lhsT=fT[:, c, :], rhs=chd[:],
                                                  start=True, stop=True), ['fT', 'chd'], ['psx'])
                S.op('act', lambda g: g.copy(out=xcs[:], in_=psx[:]), ['psx'], ['xcs'])
                S.dma('act', XCS_d[s, i], xcs[:], reads=['xcs'])
                if KCUT <= 3:
                    continue
                for k in range(8):
                    S.op('pe', lambda g: g.matmul(psq[:, 0:512], lhsT=hT[:, k, :], rhs=W1[:, k, 512:1024],
                                                  start=(k == 0), stop=(k == 7)), ['W1', 'hT'], ['psq0'])
                for k in range(8):
                    S.op('pe', lambda g: g.matmul(psq[:, 512:640], lhsT=hT[:, k, :], rhs=W1[:, k, 1024:1152],
                                                  start=(k == 0), stop=(k == 7)), ['W1', 'hT'], ['psq1'])
                for k in range(8):
                    S.op('pe', lambda g: g.matmul(psv[:, 0:128], lhsT=hT[:, k, :], rhs=W1[:, k, 1152:1280],
                                                  start=(k == 0), stop=(k == 7)), ['W1', 'hT'], ['psv'])
                S.op('act', lambda g: g.activation(out=sqq[:], in_=psq[:, 0:640], func=AF.Square), ['psq0', 'psq1'], ['sqq'])
                S.op('dve', lambda g: g.tensor_reduce(out=ssq[:], in_=sqq[:].rearrange("p (h d) -> p h d", d=64),
                                                      axis=AX.X, op=ALU.add), ['sqq'], ['ssq'])
                S.op('dve', lambda g: g.tensor_scalar(out=ssq[:], in0=ssq[:], scalar1=1.0 / 64, scalar2=EPS,
                                                      op0=ALU.mult, op1=ALU.add), ['ssq'], ['ssq'])
                S.op('act', lambda g: g.activation(out=ssq[:], in_=ssq[:], func=AF.Sqrt), ['ssq'], ['ssq'])
                S.op('dve', lambda g: g.reciprocal(out=rq[:], in_=ssq[:]), ['ssq'], ['rq'])
                S.op('dve', lambda g: g.tensor_tensor(out=qn[:].rearrange("p (h d) -> p h d", d=64),
                                                      in0=psq[:, 0:640].rearrange("p (h d) -> p h d", d=64),
                                                      in1=rq[:].unsqueeze(2).broadcast_to([128, 10, 64]), op=ALU.mult),
                     ['psq0', 'psq1', 'rq'], ['qn'])
                if KCUT <= 4:
                    continue
                if i < 16:
                    S.op('dve', lambda g: g.tensor_tensor(out=qn[:], in0=qn[:], in1=gq[:], op=ALU.mult), ['qn', 'gq'], ['qn'])
                    qv = qn[:].rearrange("p (h a b f) -> p h a b f", h=10, a=2, b=2)
                    qo = qr[:].rearrange("p (h a b f) -> p h a b f", h=10, a=2, b=2)
                    x1, x2 = qv[:, :, :, 0, :], qv[:, :, :, 1, :]
                    cosb = rc[:, i, :].rearrange("p (a f) -> p a f", a=2).unsqueeze(1).broadcast_to([128, 10, 2, 16])
                    sinb = rs[:, i, :].rearrange("p (a f) -> p a f", a=2).unsqueeze(1).broadcast_to([128, 10, 2, 16])
                    v4 = lambda t: t[:].rearrange("p (h a f) -> p h a f", h=10, a=2)
                    S.op('dve', lambda g: g.tensor_tensor(out=v4(ta), in0=x1, in1=cosb, op=ALU.mult), ['qn', 'rope'], ['ta'])
                    S.op('dve', lambda g: g.tensor_tensor(out=v4(tb), in0=x2, in1=sinb, op=ALU.mult), ['qn', 'rope'], ['tb'])
                    S.op('dve', lambda g: g.tensor_tensor(out=v4(tcx), in0=x2, in1=cosb, op=ALU.mult), ['qn', 'rope'], ['tcx'])
                    S.op('dve', lambda g: g.tensor_tensor(out=v4(td), in0=x1, in1=sinb, op=ALU.mult), ['qn', 'rope'], ['td'])
                    S.op('dve', lambda g: g.tensor_tensor(out=qo[:, :, :, 0, :], in0=v4(ta), in1=v4(tb), op=ALU.subtract),
                         ['ta', 'tb'], ['qr'])
                    S.op('dve', lambda g: g.tensor_tensor(out=qo[:, :, :, 1, :], in0=v4(tcx), in1=v4(td), op=ALU.add),
                         ['tcx', 'td'], ['qr'])
                else:
                    S.op('dve', lambda g: g.tensor_tensor(out=qr[:], in0=qn[:], in1=gq[:], op=ALU.mult), ['qn', 'gq'], ['qr'])
                if KCUT <= 5:
                    continue
                for h in range(8):
                    S.op('pe', lambda g: g.transpose(out=psqT[:, h, :], in_=qr[:, h * 64:(h + 1) * 64], identity=identb[:]),
                         ['qr', 'identb'], ['psqT'])
                for h in range(2):
                    S.op('pe', lambda g: g.transpose(out=pskT[:, h, :], in_=qr[:, 512 + h * 64:512 + (h + 1) * 64],
                                                     identity=identb[:]), ['qr', 'identb'], ['pskT'])
                S.op('dve', lambda g: g.tensor_copy(out=qTs[:], in_=psqT[:]), ['psqT'], ['qTs'])
                S.op('act', lambda g: g.copy(out=kTs[:], in_=pskT[:, 0:2, :]), ['pskT'], ['kTs'])
                S.dma('act', qT_d[s, :, :, t0:t0 + 128], qTs[:], reads=['qTs'])
                S.dma('act', kT_d[s, :, :, t0:t0 + 128], kTs[:], reads=['kTs'])
                V = v1[j]
                S.op('dve', lambda g: g.tensor_copy(out=V[:].rearrange("p (g e) -> p g e", g=2)[:, :, 0:64],
                                                    in_=psv[:, 0:128].rearrange("p (g e) -> p g e", g=2)), ['psv'], [('v1', j)])
                S.dma('act', v1_d[s, :, i, :], V[:], reads=[('v1', j)])

        def ph_f(T, P, l, last):
            xcs = T([128, 2, NT, 512], BF)
            cl = [T([128, 16, 512], BF) for _ in range(2)]
            sl = [T([128, 16, 512], BF) for _ in range(2)]
            c2 = T([128, 2, 256], BF)
            s2 = T([128, 2, 256], BF)
            zt = [T([128, 512], BF) for _ in range(2)]
            psZ = [P([128, 512]) for _ in range(2)]
            for s in range(2):
                for t3 in range(0, NT, 3):
                    S.dma('sp', xcs[:, s, t3:t3 + 3, :], XCS_d[s, t3:t3 + 3].rearrange("t p c -> p t c"), writes=['xcs'])
            S.dma('sp', c2[:], dftc2.rearrange("(k p) n -> p k n", p=128), writes=['c2'])
            S.dma('sp', s2[:], dfts2.rearrange("(k p) n -> p k n", p=128), writes=['c2'])
            n = 0
            for pb in range(4):
                jb = pb % 2
                for k4 in range(4):
                    S.dma('sp', cl[jb][:, k4 * 4:(k4 + 1) * 4, :],
                          dftc[k4 * 512:(k4 + 1) * 512, pb * 512:(pb + 1) * 512].rearrange("(k p) n -> p k n", p=128),
                          writes=[('cl', jb)])
                    S.dma('sp', sl[jb][:, k4 * 4:(k4 + 1) * 4, :],
                          dfts[k4 * 512:(k4 + 1) * 512, pb * 512:(pb + 1) * 512].rearrange("(k p) n -> p k n", p=128),
                          writes=[('cl', jb)])
                for s in range(2):
                    for c in range(2):
                        j = n % 2
                        n += 1
                        for kt in range(16):
                            S.op('pe', lambda g: g.matmul(psZ[j][:], lhsT=xcs[:, s, kt, c * 256:c * 256 + 128],
                                                          rhs=cl[jb][:, kt, :], start=(kt == 0), stop=False),
                                 ['xcs', ('cl', jb)], [('psZ', j)])
                            S.op('pe', lambda g: g.matmul(psZ[j][:], lhsT=xcs[:, s, kt, c * 256 + 128:c * 256 + 256],
                                                          rhs=sl[jb][:, kt, :], start=False, stop=(kt == 15)),
                                 ['xcs', ('cl', jb)], [('psZ', j)])
                        S.op('act', lambda g: g.activation(out=zt[j][:], in_=psZ[j][:], func=AF.Copy,
                                                           scale=float((L * 64) ** -0.5)), [('psZ', j)], [('zt', j)])
                        S.dma('act', ZT_d[s, c, :, pb * 512:(pb + 1) * 512], zt[j][:], reads=[('zt', j)])
            if not last:
                for s in range(2):
                    for c in range(2):
                        j = n % 2
                        n += 1
                        for kt in range(2):
                            S.op('pe', lambda g: g.matmul(psZ[j][:, 0:256], lhsT=xcs[:, s, 16 + kt, c * 256:c * 256 + 128],
                                                          rhs=c2[:, kt, :], start=(kt == 0), stop=False),
                                 ['xcs', 'c2'], [('psZ', j)])
                            S.op('pe', lambda g: g.matmul(psZ[j][:, 0:256],
                                                          lhsT=xcs[:, s, 16 + kt, c * 256 + 128:c * 256 + 256],
                                                          rhs=s2[:, kt, :], start=False, stop=(kt == 1)),
                                 ['xcs', 'c2'], [('psZ', j)])
                        S.op('act', lambda g: g.activation(out=zt[j][:, 0:256], in_=psZ[j][:, 0:256], func=AF.Copy,
                                                           scale=float((LC * 64) ** -0.5)), [('psZ', j)], [('zt', j)])
                        S.dma('act', ZT_d[s, c, :, L:LT], zt[j][:, 0:256], reads=[('zt', j)])

        def ph_p(T, P, l, last):
            invl = T([128, 2, L])
            invx = T([128, 2, LC])
            PL = T([128, L + 32])
            PX = T([128, LC + 32])
            A = T([128, L + 32])
            B = T([128, L + 32])
            tmp = T([128, L])
            po = T([128, L], BF)
            S.dma('sp', invl[:], invl_d.rearrange("c p t -> p c t"), writes=['inv'])
            S.dma('sp', invx[:], invx_d.rearrange("c p t -> p c t"), writes=['inv'])
            S.op('dve', lambda g: g.memset(PL[:], 0.0), (), ['PL'])
            S.op('dve', lambda g: g.memset(PX[:], 0.0), (), ['PX'])
            for s in range(2):
                for kind in range(1 if last else 2):
                    for c in range(2):
                        if kind == 0:
                            Pb, pk, n, t0, inv = PL, 'PL', L, 0, invl
                        else:
                            Pb, pk, n, t0, inv = PX, 'PX', LC, L, invx
                        S.dma('sp', Pb[:, 16:16 + n], pT_d[s, c, :, t0:t0 + n], writes=[pk])
                        tt = lambda e, o, a, b, rd, wr: S.op(e, lambda g: g.tensor_tensor(out=o, in0=a, in1=b, op=ALU.add), rd, wr)
                        tt('dve', A[:, 1:n + 31], Pb[:, 1:n + 31], Pb[:, 0:n + 30], [pk], ['A'])
                        tt('dve', B[:, 2:n + 30], A[:, 3:n + 31], A[:, 1:n + 29], ['A'], ['B'])
                        if c == 1:
                            tt('dve', A[:, 4:n + 28], B[:, 6:n + 30], B[:, 2:n + 26], ['B'], ['A'])
                            tt('dve', B[:, 8:n + 24], A[:, 12:n + 28], A[:, 4:n + 20], ['A'], ['B'])
                        S.op('dve', lambda g: g.tensor_tensor(out=tmp[0:64, 0:n], in0=A[0:64, 16:16 + n], in1=inv[0:64, c, :],
                                                              op=ALU.mult), ['A', 'inv'], ['tmp0'])
                        S.op('dve', lambda g: g.tensor_tensor(out=tmp[64:128, 0:n], in0=B[64:128, 16:16 + n],
                                                               in1=inv[64:128, c, :], op=ALU.mult), ['B', 'inv'], ['tmp1'])
                        S.op('dve', lambda g: g.tensor_tensor(out=po[:, 0:n], in0=tmp[:, 0:n], in1=Pb[:, 16:16 + n],
                                                              op=ALU.subtract), ['tmp0', 'tmp1', pk], ['po'])
                        S.dma('act', plT_d[s, c, :, t0:t0 + n], po[:, 0:n], reads=['po'])

        def ph_a(T, P, l, last):
            identb = T([128, 128], BF)
            kT = [T([128, 2, LT], BF) for _ in range(2)]
            qT = [T([128, 8, LT], BF) for _ in range(2)]
            v1 = [T([128, NT, 130], BF) for _ in range(2)]
            Pm = [T([128, 512], BF) for _ in range(3)]
            ya = [T([128, 512], BF) for _ in range(2)]
            yaT = [T([128, 4, 128], BF) for _ in range(2)]
            rden = T([128, 8])
            acc = P([128, 4, 512])
            psS = [P([128, 512]) for _ in range(2)]
            psY = P([128, 4, 128], BF)
            S.dma('sp', identb[:], identb_d, writes=['identb'])
            for s in range(2):
                S.op('dve', lambda g: g.memset(kT[s][64:128], 0.0), (), [('kT', s)])
                S.op('dve', lambda g: g.memset(qT[s][64:128], 0.0), (), [('qT', s)])
                S.dma('sp', kT[s][0:64], kT_d[s], writes=[('kT', s)])
                S.dma('sp', v1[s][:], v1_d[s], writes=[('v1', s)])
                S.dma('sp', qT[s][0:64], qT_d[s], writes=[('qT', s)])
            n = 0
            nq = 0
            for s in range(2):
                jobs = [(qt, list(range(NT))) for qt in range(16)]
                if not last:
                    jobs += [(qt, [16, 17]) for qt in (16, 17)]
                for qt, kts in jobs:
                    jq = nq % 2
                    nq += 1
                    for gi in range(2):
                        for ki, kt in enumerate(kts):
                            j = n % 2
                            j3 = n % 3
                            n += 1
                            S.op('pe', lambda g: g.matmul(psS[j][:], lhsT=kT[s][:, gi, kt * 128:(kt + 1) * 128],
                                                          rhs=qT[s][:, 4 * gi:4 * gi + 4, qt * 128:(qt + 1) * 128],
                                                          start=True, stop=True),
                                 [('kT', s), ('qT', s)], [('psS', j)])
                            S.op('act', lambda g: g.activation(out=Pm[j3][:], in_=psS[j][:], func=AF.Exp, scale=0.125),
                                 [('psS', j)], [('Pm', j3)])
                            for hh in range(4):
                                S.op('pe', lambda g: g.matmul(acc[:, hh, 0:65], lhsT=Pm[j3][:, hh * 128:(hh + 1) * 128],
                                                              rhs=v1[s][:, kt, gi * 65:(gi + 1) * 65],
                                                              start=(ki == 0), stop=(ki == len(kts) - 1)),
                                     [('Pm', j3), ('v1', s)], ['psacc'])
                        S.op('dve', lambda g: g.reciprocal(out=rden[:, 4 * gi:4 * gi + 4], in_=acc[:, :, 64]), ['psacc'], ['rden'])
                        S.op('dve', lambda g: g.tensor_tensor(
                            out=ya[jq][:, gi * 256:(gi + 1) * 256].rearrange("p (h d) -> p h d", d=64),
                            in0=acc[:, :, 0:64],
                            in1=rden[:, 4 * gi:4 * gi + 4].unsqueeze(2).broadcast_to([128, 4, 64]), op=ALU.mult),
                            ['psacc', 'rden'], [('ya', jq)])
                    for c in range(4):
                        S.op('pe', lambda g: g.transpose(out=psY[:, c, :], in_=ya[jq][:, c * 128:(c + 1) * 128],
                                                         identity=identb[:]), [('ya', jq), 'identb'], ['psY'])
                    S.op('dve', lambda g: g.tensor_copy(out=yaT[jq][:], in_=psY[:]), ['psY'], [('yaT', jq)])
                    S.dma('sp', yaT_d[s, :, :, qt * 128:(qt + 1) * 128].rearrange("c p t -> p c t"), yaT[jq][:],
                          reads=[('yaT', jq)])

        def ph_m3(T, P, l, last):
            WG = T([128, 8, 3072], BF)
            fw = T([128, 2, D], BF)
            ppj = T([128, 2, D], BF)
            apj = T([128, 4, D], BF)
            wo = T([128, 8, D], BF)
            pwf = T([128, 2, 128])
            pwbd = T([128, 2, 128], BF)
            psc = T([128, 2])
            stg = [T([128, 1536]) for _ in range(2)]
            g2b = [T([128, D]) for _ in range(3)]
            hTb = [T([128, 8, 512], BF) for _ in range(2)]
            ZTb = [T([128, 2, 512], BF) for _ in range(2)]
            plb = [T([128, 2, 512], BF) for _ in range(2)]
            yab = [T([128, 4, 512], BF) for _ in range(2)]
            mx = T([128, 2, 512], BF)
            sg = [T([128, 512]) for _ in range(3)]
            tq = [T([128, 512]) for _ in range(3)]
            uT = T([128, 8, 512], BF)
            xt = [T([128, D]) for _ in range(2)]
            tmo = T([128, D])
            xn = [T([128, D]) for _ in range(2)]
            psM = P([128, 512])
            psA = P([128, 512])
            psB = P([128, 512])
            psC = P([128, 512])
            psG = [P([128, 512]) for _ in range(2)]
            psO = [P([128, 512]) for _ in range(2)]

            S.op('dve', lambda g: g.memset(pwf[:], 0.0), (), ['pwf'])
            for gidx in range(4):
                hh = gidx % 2
                S.dma('sp', pwf[hh * 64:(hh + 1) * 64, gidx // 2, hh * 64:(hh + 1) * 64], pool_w[l, gidx], writes=['pwf'])
            S.op('dve', lambda g: g.tensor_copy(out=pwbd[:], in_=pwf[:]), ['pwf'], ['pwbd'])
            S.dma('sp', psc[:], pool_scale[l].rearrange("(c p) -> p c", p=128), writes=['psc'], allow_slow_non_contiguous=True)
            for r in range(3):
                S.dma('sp', g2b[r][:], modv_d[l, r, 2:3, :].broadcast_to([128, D]), writes=['g2b'])
            load_w_bf(T, WG, 'WG', lambda k: w_in[l, k * 128:(k + 1) * 128, 1280:4352], 3072, stg, 8)
            load_w_bf(T, fw, 'fw', lambda k: fourier_w[l, k * 128:(k + 1) * 128, :], D, stg, 2)
            load_w_bf(T, ppj, 'ppj', lambda k: pool_proj[l, k * 128:(k + 1) * 128, :], D, stg, 2)
            load_w_bf(T, apj, 'apj', lambda k: attn_proj[l, k * 128:(k + 1) * 128, :], D, stg, 4)
            load_w_bf(T, wo, 'wo', lambda k: w_out[l, k * 128:(k + 1) * 128, :], D, stg, 8)

            blocks = []
            for s in range(2):
                for b in range(4):
                    blocks.append((s, b * 512, 512))
                if not last:
                    blocks.append((s, L, LC))
            ng = 0
            nx = 0
            for bi, (s, t0, nb) in enumerate(blocks):
                jb = bi % 2
                r = s if t0 < L else 2
                for k2 in range(2):
                    S.dma('sp', hTb[jb][:, k2 * 4:(k2 + 1) * 4, 0:nb],
                          hT_d[s, k2 * 4:(k2 + 1) * 4, :, t0:t0 + nb].rearrange("k p t -> p k t"), writes=[('hTb', jb)])
                S.dma('sp', ZTb[jb][:, :, 0:nb], ZT_d[s, :, :, t0:t0 + nb].rearrange("k p t -> p k t"), writes=[('ZTb', jb)])
                S.dma('sp', plb[jb][:, :, 0:nb], plT_d[s, :, :, t0:t0 + nb].rearrange("k p t -> p k t"), writes=[('plb', jb)])
                S.dma('sp', yab[jb][:, :, 0:nb], yaT_d[s, :, :, t0:t0 + nb].rearrange("k p t -> p k t"), writes=[('yab', jb)])
                for c in range(2):
                    S.op('pe', lambda g: g.matmul(psM[:, 0:nb], lhsT=pwbd[:, c, :], rhs=plb[jb][:, c, 0:nb],
                                                  start=True, stop=True), ['pwbd', ('plb', jb)], ['psM'])
                    S.op('act', lambda g: g.activation(out=mx[:, c, 0:nb], in_=psM[:, 0:nb], func=AF.Copy,
                                                       scale=psc[:, c:c + 1]), ['psM', 'psc'], ['mx'])
                for fc in range(8):
                    fs = slice(fc * 128, (fc + 1) * 128)
                    for k in range(2):
                        S.op('pe', lambda g: g.matmul(psA[:, 0:nb], lhsT=fw[:, k, fs], rhs=ZTb[jb][:, k, 0:nb],
                                                      start=(k == 0), stop=(k == 1)), ['fw', ('ZTb', jb)], ['psA'])
                    for k in range(2):
                        S.op('pe', lambda g: g.matmul(psB[:, 0:nb], lhsT=ppj[:, k, fs], rhs=mx[:, k, 0:nb],
                                                      start=(k == 0), stop=(k == 1)), ['ppj', 'mx'], ['psB'])
                    for k in range(4):
                        S.op('pe', lambda g: g.matmul(psC[:, 0:nb], lhsT=apj[:, k, fs], rhs=yab[jb][:, k, 0:nb],
                                                      start=(k == 0), stop=(k == 3)), ['apj', ('yab', jb)], ['psC'])
                    for br, pY in enumerate((psA, psB, psC)):
                        jg = ng % 2
                        ng += 1
                        for k in range(8):
                            S.op('pe', lambda g: g.matmul(psG[jg][:, 0:nb], lhsT=WG[:, k, br * D + fc * 128:br * D + (fc + 1) * 128],
                                                          rhs=hTb[jb][:, k, 0:nb], start=(k == 0), stop=(k == 7)),
                                 ['WG', ('hTb', jb)], [('psG', jg)])
                        S.op('act', lambda g: g.activation(out=sg[br][:, 0:nb], in_=psG[jg][:, 0:nb], func=AF.Sigmoid),
                             [('psG', jg)], [('sg', br)])
                        S.op('dve', lambda g: g.tensor_tensor(out=tq[br][:, 0:nb], in0=sg[br][:, 0:nb], in1=pY[:, 0:nb],
                                                              op=ALU.mult), [('sg', br), ('psA', 'psB', 'psC')[br]], [('tq', br)])
                    S.op('dve', lambda g: g.tensor_tensor(out=tq[0][:, 0:nb], in0=tq[0][:, 0:nb], in1=tq[1][:, 0:nb], op=ALU.add),
                         [('tq', 0), ('tq', 1)], [('tq', 0)])
                    S.op('dve', lambda g: g.tensor_tensor(out=uT[:, fc, 0:nb], in0=tq[0][:, 0:nb], in1=tq[2][:, 0:nb], op=ALU.add),
                         [('tq', 0), ('tq', 2)], ['uT'])
                for jt in range(nb // 128):
                    i = (t0 + jt * 128) // 128
                    jx = nx % 2
                    nx += 1
                    S.dma('sp', xt[jx][:], xsrc(l, s, i), writes=[('xt', jx)])
                    for half in range(2):
                        for k in range(8):
                            S.op('pe', lambda g: g.matmul(psO[half][:], lhsT=uT[:, k, jt * 128:(jt + 1) * 128],
                                                          rhs=wo[:, k, half * 512:(half + 1) * 512],
                                                          start=(k == 0), stop=(k == 7)), ['uT', 'wo'], [('psO', half)])
                        S.op('dve', lambda g: g.tensor_tensor(out=tmo[:, half * 512:(half + 1) * 512], in0=psO[half][:],
                                                              in1=g2b[r][:, half * 512:(half + 1) * 512], op=ALU.mult),
                             [('psO', half), 'g2b'], [('tmo', half)])
                    S.op('dve', lambda g: g.tensor_tensor(out=xn[jx][:], in0=tmo[:], in1=xt[jx][:], op=ALU.add),
                         [('tmo', 0), ('tmo', 1), ('xt', jx)], [('xn', jx)])
                    S.dma('act', xr[s, i * 128:(i + 1) * 128, :], xn[jx][:], reads=[('xn', jx)])


        def ph_conv(T, P, l):
            NR = 3
            st = [T([128, 8, D]) for _ in range(NR)]
            sb = [T([128, 8, D], BF) for _ in range(NR)]
            n = 0
            for src, dst in ((peer_u, ub_d), (peer_v, vb_d)):
                for ch in range(16):
                    j = n % NR
                    n += 1
                    rows = slice(ch * 1024, (ch + 1) * 1024)
                    S.dma('sp', st[j][:], src[l, rows, :].rearrange("(p r) d -> p r d", r=8), writes=[('cst', j)])
                    cast(sb[j][:], st[j][:], [('cst', j)], [('csb', j)])
                    S.dma('act', dst[rows, :].rearrange("(p r) d -> p r d", r=8), sb[j][:], reads=[('csb', j)])

        def ph_peer(T, P, l, last):
            NB = 10
            wq = T([128, 8, 2048], BF)
            stg = [T([128, 2048]) for _ in range(2)]
            kf = T([128, 16, 128])
            keysT = T([128, 16, 128], BF)
            identb = T([128, 128], BF)
            identf = T([128, 128])
            iota = T([128, 16])
            mbs = [[T([128, D]) for _ in range(3)] for _ in range(2)]
            xt = [T([128, D]) for _ in range(2)]
            junk = T([128, D])
            h2 = T([128, D])
            h2b = T([128, D], BF)
            h2T = T([128, 8, 128], BF)
            qTs = T([128, 16, 128], BF)
            sc = T([128, 16, 128])
            wk = T([128, 256])
            V = T([128, 16, 16])
            I = T([128, 16, 16], U32)
            If = T([128, 16, 16])
            cand = T([128, 8, 256])
            Tt = T([128, 8, 16])
            Pp = T([128, 8, 16], U32)
            pa = T([128, 8, 16], U32)
            pbb = T([128, 8, 16], U32)
            paf = T([128, 8, 16])
            pbf = T([128, 8, 16])
            eq = T([128, 8, 16, 16])
            ix1 = T([128, 8, 16])
            ix2 = T([128, 8, 16])
            ef = T([128, 128])
            eidx = [T([128, 128], U32) for _ in range(2)]
            dd = T([128, 8, 16])
            zz = T([128, 8])
            gg = T([128, 128])
            araw = T([128, 128])
            g1 = T([128, 128])
            g2 = T([128, 128])
            wgt = T([128, 128])
            ss = T([128, 4])
            Ug = [T([128, D], BF) for _ in range(NB)]
            Vg = [T([128, D], BF) for _ in range(NB)]
            uj = T([128, D], BF)
            acc = T([128, D])
            xn = [T([128, D]) for _ in range(2)]
            psQ = P([128, 16, 128])
            psV0 = None
            psT = P([128, 8, 128], BF)
            psK = P([128, 4, 128])
            psV = P([128, 1024])
            dg = [T([128, 128], BF) for _ in range(NB)]

            S.dma('sp', identb[:], identb_d, writes=['identb'])
            S.dma('sp', identf[:], identf_d, writes=['identf'])
            S.dma('sp', iota[:], iota_d, writes=['iota'])
            for h4 in range(4):
                S.dma('sp', kf[:, h4 * 4:(h4 + 1) * 4, :], peer_keys[l, 2 * h4:2 * h4 + 2].rearrange("h p n d -> n (h p) d"),
                      writes=['kf'])
            for hp in range(16):
                S.op('pe', lambda g: g.transpose(out=psK[:, hp % 4, :], in_=kf[:, hp, :], identity=identf[:]),
                     ['kf', 'identf'], ['psK'])
                if hp % 4 == 3:
                    S.op('dve', lambda g: g.tensor_copy(out=keysT[:, hp - 3:hp + 1, :], in_=psK[:]), ['psK'], ['keysT'])
            load_w_bf(T, wq, 'wq', lambda k: peer_wq[l, k * 128:(k + 1) * 128, :], 2048, stg, 8)

            order = [(0, i) for i in range(16)] + [(1, i) for i in range(16)]
            if not last:
                order += [(0, 16), (0, 17), (1, 16), (1, 17)]
            order = order[:KPT]
            cur = {'r': -1, 'set': 1}
            tset = {}
            ctr = {'u': 0, 'v': 0}
            S.dma('sp', xt[0][:], xr[order[0][0], order[0][1] * 128:(order[0][1] + 1) * 128, :], writes=[('xt', 0)])

            def prologue(n):
                s, i = order[n]
                j = n % 2
                r = s if i < 16 else 2
                if r != cur['r']:
                    cur['set'] = 1 - cur['set']
                    for jj, col in enumerate((3, 4, 5)):
                        S.dma('sp', mbs[cur['set']][jj][:], modv_d[l, r, col:col + 1, :].broadcast_to([128, D]),
                              writes=[('mb', cur['set'])])
                    cur['r'] = r
                tset[n] = cur['set']
                mb = mbs[tset[n]]
                mbk = ('mb', tset[n])
                X = xt[j]
                S.op('act', lambda g: g.activation(out=junk[:], in_=X[:], func=AF.Square), [('xt', j)], ['junk'])
                S.op('dve', lambda g: g.tensor_reduce(out=ss[:, 0:1], in_=junk[:], axis=AX.X, op=ALU.add), ['junk'], ['ss'])
                S.op('dve', lambda g: g.tensor_scalar(out=ss[:, 1:2], in0=ss[:, 0:1], scalar1=1.0 / D, scalar2=EPS,
                                                      op0=ALU.mult, op1=ALU.add), ['ss'], ['ss'])
                S.op('act', lambda g: g.activation(out=ss[:, 2:3], in_=ss[:, 1:2], func=AF.Sqrt), ['ss'], ['ss'])
                S.op('dve', lambda g: g.reciprocal(out=ss[:, 3:4], in_=ss[:, 2:3]), ['ss'], ['ss'])
                S.op('dve', lambda g: g.scalar_tensor_tensor(out=junk[:], in0=X[:], scalar=ss[:, 3:4], op0=ALU.mult,
                                                             in1=mb[0][:], op1=ALU.mult), [('xt', j), 'ss', mbk, 'junk'], ['junk'])
                S.op('dve', lambda g: g.tensor_tensor(out=h2[:], in0=junk[:], in1=mb[1][:], op=ALU.add), ['junk', mbk], ['h2'])
                S.op('act', lambda g: g.copy(out=h2b[:], in_=h2[:]), ['h2'], ['h2b'])
                for k in range(8):
                    S.op('pe', lambda g: g.transpose(out=psT[:, k, :], in_=h2b[:, k * 128:(k + 1) * 128], identity=identb[:]),
                         ['h2b', 'identb'], ['psT'])
                S.op('act', lambda g: g.copy(out=h2T[:], in_=psT[:]), ['psT'], ['h2T'])
                for hp in range(16):
                    for k in range(8):
                        S.op('pe', lambda g: g.matmul(psQ[:, hp, :], lhsT=wq[:, k, hp * 128:(hp + 1) * 128], rhs=h2T[:, k, :],
                                                      start=(k == 0), stop=(k == 7)), ['wq', 'h2T'], [('psQ', hp // 8)])
                S.op('dve', lambda g: g.tensor_copy(out=qTs[:, 0:8, :], in_=psQ[:, 0:8, :]), [('psQ', 0)], ['qTs'])
                S.op('act', lambda g: g.copy(out=qTs[:, 8:16, :], in_=psQ[:, 8:16, :]), [('psQ', 1)], ['qTs'])
                for hp in range(16):
                    S.op('pe', lambda g: g.matmul(psQ[:, hp, :], lhsT=qTs[:, hp, :], rhs=keysT[:, hp, :], start=True, stop=True),
                         ['qTs', 'keysT'], [('psQ', hp // 8)])
                S.op('dve', lambda g: g.tensor_copy(out=sc[:, 0:8, :], in_=psQ[:, 0:8, :]), [('psQ', 0)], ['sc'])
                S.op('act', lambda g: g.copy(out=sc[:, 8:16, :], in_=psQ[:, 8:16, :]), [('psQ', 1)], ['sc'])
                for hp in range(16):
                    S.op('dve', lambda g: g.max(out=V[:, hp, 0:8], in_=sc[:, hp, :]), ['sc'], ['V'])
                    S.op('dve', lambda g: g.max_index(out=I[:, hp, 0:8], in_max=V[:, hp, 0:8], in_values=sc[:, hp, :]),
                         ['sc', 'V'], ['I'])
                    S.op('dve', lambda g: g.match_replace(out=wk[:, 0:128], in_to_replace=V[:, hp, 0:8], in_values=sc[:, hp, :],
                                                          imm_value=NEG), ['sc', 'V'], ['wk'])
                    S.op('dve', lambda g: g.max(out=V[:, hp, 8:16], in_=wk[:, 0:128]), ['wk'], ['V'])
                    S.op('dve', lambda g: g.max_index(out=I[:, hp, 8:16], in_max=V[:, hp, 8:16], in_values=wk[:, 0:128]),
                         ['wk', 'V'], ['I'])
                S.op('dve', lambda g: g.tensor_copy(out=If[:], in_=I[:]), ['I'], ['If'])
                V4 = V[:].rearrange("p (h t) k -> p h t k", t=2)
                I4 = If[:].rearrange("p (h t) k -> p h t k", t=2)
                S.op('dve', lambda g: g.tensor_tensor(
                    out=cand[:].rearrange("p h (a b) -> p h a b", b=16),
                    in0=V4[:, :, 0, :].unsqueeze(3).broadcast_to([128, 8, 16, 16]),
                    in1=V4[:, :, 1, :].unsqueeze(2).broadcast_to([128, 8, 16, 16]), op=ALU.add), ['V'], ['cand'])
                for h in range(8):
                    S.op('dve', lambda g: g.max(out=Tt[:, h, 0:8], in_=cand[:, h, :]), ['cand'], ['Tt'])
                    S.op('dve', lambda g: g.max_index(out=Pp[:, h, 0:8], in_max=Tt[:, h, 0:8], in_values=cand[:, h, :]),
                         ['cand', 'Tt'], ['Pp'])
                    S.op('dve', lambda g: g.match_replace(out=wk[:], in_to_replace=Tt[:, h, 0:8], in_values=cand[:, h, :],
                                                          imm_value=NEG), ['cand', 'Tt'], ['wk'])
                    S.op('dve', lambda g: g.max(out=Tt[:, h, 8:16], in_=wk[:]), ['wk'], ['Tt'])
                    S.op('dve', lambda g: g.max_index(out=Pp[:, h, 8:16], in_max=Tt[:, h, 8:16], in_values=wk[:]),
                         ['wk', 'Tt'], ['Pp'])
                S.op('dve', lambda g: g.tensor_single_scalar(out=pa[:], in_=Pp[:], scalar=4, op=ALU.logical_shift_right),
                     ['Pp'], ['pa'])
                S.op('dve', lambda g: g.tensor_single_scalar(out=pbb[:], in_=Pp[:], scalar=15, op=ALU.bitwise_and),
                     ['Pp'], ['pbb'])
                S.op('dve', lambda g: g.tensor_copy(out=paf[:], in_=pa[:]), ['pa'], ['paf'])
                S.op('dve', lambda g: g.tensor_copy(out=pbf[:], in_=pbb[:]), ['pbb'], ['pbf'])
                iob = iota[:].unsqueeze(1).unsqueeze(1).broadcast_to([128, 8, 16, 16])
                for (pf, pk, tsel, outi, ok) in ((paf, 'paf', 0, ix1, 'i1'), (pbf, 'pbf', 1, ix2, 'i2')):
                    S.op('dve', lambda g: g.tensor_tensor(out=eq[:], in0=pf[:].unsqueeze(3).broadcast_to([128, 8, 16, 16]),
                                                          in1=iob, op=ALU.is_equal), [pk, 'iota', 'eq'], ['eq'])
                    S.op('dve', lambda g: g.tensor_tensor(out=eq[:], in0=eq[:],
                                                          in1=I4[:, :, tsel, :].unsqueeze(2).broadcast_to([128, 8, 16, 16]),
                                                          op=ALU.mult), ['eq', 'If'], ['eq'])
                    S.op('dve', lambda g: g.tensor_reduce(out=outi[:], in_=eq[:], axis=AX.X, op=ALU.add), ['eq'], [ok])
                E = eidx[j]
                S.op('dve', lambda g: g.scalar_tensor_tensor(out=ef[:].rearrange("p (h k) -> p h k", k=16), in0=ix1[:], scalar=128.0,
                                                             op0=ALU.mult, in1=ix2[:], op1=ALU.add), ['i1', 'i2'], ['ef'])
                S.op('dve', lambda g: g.tensor_copy(out=E[:], in_=ef[:]), ['ef'], [('eidx', j)])
                if eidx_d is not None:
                    S.dma('act', eidx_d[n], E[:], reads=[('eidx', j)])
                S.op('dve', lambda g: g.tensor_tensor(out=dd[:], in0=Tt[:], in1=Tt[:, :, 0:1].broadcast_to([128, 8, 16]),
                                                      op=ALU.subtract), ['Tt'], ['dd'])
                S.op('act', lambda g: g.activation(out=dd[:], in_=dd[:], func=AF.Exp), ['dd'], ['dd'])
                S.op('dve', lambda g: g.tensor_reduce(out=zz[:], in_=dd[:], axis=AX.X, op=ALU.add), ['dd'], ['zz'])
                S.op('dve', lambda g: g.reciprocal(out=zz[:], in_=zz[:]), ['zz'], ['zz'])
                S.op('dve', lambda g: g.tensor_tensor(out=gg[:].rearrange("p (h k) -> p h k", k=16), in0=dd[:],
                                                      in1=zz[:].unsqueeze(2).broadcast_to([128, 8, 16]), op=ALU.mult),
                     ['dd', 'zz'], ['gg'])

            def uphase(n):
                s, i = order[n]
                j = n % 2
                E = eidx[j]
                ngu = ctr['u']
                for jj in range(128):
                    b = ngu % NB
                    ngu += 1
                    S.gather(Ug[b][:], ub_d, E[:, jj:jj + 1], reads=[('eidx', j)], writes=[('Ug', b)])
                    S.op('dve', lambda g: g.scalar_tensor_tensor(out=uj[:], in0=Ug[b][:], scalar=1.0, op0=ALU.mult,
                                                                 in1=h2b[:], op1=ALU.mult, accum_out=araw[:, jj:jj + 1]),
                         [('Ug', b), 'h2b'], ['uj', 'araw'])
                S.op('dve', lambda g: g.tensor_tensor(out=g1[:], in0=araw[:], in1=araw[:], op=ALU.mult), ['araw'], ['g1'])
                S.op('dve', lambda g: g.tensor_scalar(out=g1[:], in0=g1[:], scalar1=0.044715, scalar2=1.0, op0=ALU.mult,
                                                      op1=ALU.add), ['g1'], ['g1'])
                S.op('dve', lambda g: g.tensor_tensor(out=g1[:], in0=g1[:], in1=araw[:], op=ALU.mult), ['g1', 'araw'], ['g1'])
                S.op('act', lambda g: g.activation(out=g2[:], in_=g1[:], func=AF.Tanh, scale=0.7978845608028654), ['g1'], ['g2'])
                S.op('dve', lambda g: g.tensor_scalar(out=g2[:], in0=g2[:], scalar1=1.0, scalar2=0.5, op0=ALU.add,
                                                      op1=ALU.mult), ['g2'], ['g2'])
                S.op('dve', lambda g: g.tensor_tensor(out=g2[:], in0=g2[:], in1=araw[:], op=ALU.mult), ['g2', 'araw'], ['g2'])
                S.op('dve', lambda g: g.tensor_tensor(out=wgt[:], in0=g2[:], in1=gg[:], op=ALU.mult), ['g2', 'gg'], ['wgt'])
                ctr['u'] = ngu

            def vphase(n):
                s, i = order[n]
                j = n % 2
                E = eidx[j]
                ngv = ctr['v']
                for jj in range(128):
                    b = ngv % NB
                    ngv += 1
                    S.gather(Vg[b][:], vb_d, E[:, jj:jj + 1], reads=[('eidx', j)], writes=[('Vg', b)])
                    S.op('act', lambda g: g.activation(out=dg[b][:], in_=identb[:], func=AF.Copy, scale=wgt[:, jj:jj + 1]),
                         ['identb', 'wgt'], [('dg', b)])
                    for half in range(2):
                        S.op('pe', lambda g: g.matmul(psV[:, half * 512:(half + 1) * 512], lhsT=dg[b][:],
                                                      rhs=Vg[b][:, half * 512:(half + 1) * 512],
                                                      start=(jj == 0), stop=(jj == 127)),
                             [('dg', b), ('Vg', b)], [('psV', half)])
                ctr['v'] = ngv

            def epilogue(n):
                s, i = order[n]
                j = n % 2
                X = xt[j]
                mb = mbs[tset[n]]
                mbk = ('mb', tset[n])
                S.op('dve', lambda g: g.tensor_tensor(out=acc[:], in0=psV[:], in1=mb[2][:], op=ALU.mult),
                     [('psV', 0), ('psV', 1), mbk], ['acc'])
                S.op('dve', lambda g: g.tensor_tensor(out=xn[j][:], in0=acc[:], in1=X[:], op=ALU.add),
                     ['acc', ('xt', j)], [('xn', j)])
                if last:
                    dst = y[s, i * 128:(i + 1) * 128, :]
                else:
                    dst = (xo if peer_only else xr)[s, i * 128:(i + 1) * 128, :]
                S.dma('act', dst, xn[j][:], reads=[('xn', j)])

            prologue(0)
            for n in range(len(order)):
                j = n % 2
                if n + 1 < len(order):
                    s2, i2 = order[n + 1]
                    S.dma('sp', xt[1 - j][:], xr[s2, i2 * 128:(i2 + 1) * 128, :], writes=[('xt', 1 - j)])
                uphase(n)
                if n + 1 < len(order):
                    prologue(n + 1)
                vphase(n)
                epilogue(n)


        phases = []
        for l in range(depth):
            last = (l == depth - 1) and not all_ctx
            phases += [(ph_mod, l), (ph_m1, l), (ph_f, l, last), (ph_p, l, last), (ph_a, l, last), (ph_m3, l, last),
                       (ph_conv, l), (ph_peer, l, last)]
        if stop_after is not None:
            phases = phases[:stop_after]
        if peer_only:
            phases = [(ph_conv, 0), (ph_peer, 0, False)]
        for p in phases:
            run_phase(p[0], *p[1:])
    return nc, S


def host_consts():
    bf = ml_dtypes.bfloat16
    t = np.arange(L, dtype=np.float64)
    ang = 2 * np.pi * ((np.outer(t, t)) % L) / L
    dftc = np.cos(ang).astype(bf)
    dfts = np.sin(ang).astype(bf)
    t2 = np.arange(LC, dtype=np.float64)
    ang2 = 2 * np.pi * ((np.outer(t2, t2)) % LC) / LC
    dftc2 = np.cos(ang2).astype(bf)
    dfts2 = np.sin(ang2).astype(bf)
    c = np.arange(64, dtype=np.float64)
    a64 = 2 * np.pi * ((np.outer(c, c)) % 64) / 64
    chd = np.zeros((128, 256), np.float64)
    for g in range(2):
        chd[g * 64:(g + 1) * 64, g * 64:(g + 1) * 64] = np.cos(a64)
        chd[g * 64:(g + 1) * 64, 128 + g * 64:128 + (g + 1) * 64] = -np.sin(a64)
    chd = chd.astype(bf)
    pos = np.arange(L)
    row = (pos // 64).astype(np.float32)
    col = (pos % 64).astype(np.float32)
    inv = (np.float32(10000.0) ** (-np.arange(16, dtype=np.float32) / np.float32(16))).astype(np.float32)
    ar = (row[:, None] * inv[None, :]).astype(np.float32)
    ac = (col[:, None] * inv[None, :]).astype(np.float32)
    ropec = np.concatenate([np.cos(ar), np.cos(ac)], axis=1).astype(np.float32)
    ropes = np.concatenate([np.sin(ar), np.sin(ac)], axis=1).astype(np.float32)

    def invcnt(n):
        out = np.zeros((2, 128, n), np.float32)
        tt = np.arange(n)
        for gi, win in enumerate((2, 4, 8, 16)):
            lo = np.maximum(tt - win // 2, 0)
            hi = np.minimum(tt + win - win // 2, n)
            out[gi // 2, (gi % 2) * 64:(gi % 2 + 1) * 64, :] = (1.0 / (hi - lo).astype(np.float32))[None, :]
        return out
    return {
        "dftc": dftc, "dfts": dfts, "dftc2": dftc2, "dfts2": dfts2, "chd": chd,
        "ropec": ropec, "ropes": ropes, "invl": invcnt(L), "invx": invcnt(LC),
        "identb": np.eye(128).astype(bf), "identf": np.eye(128, dtype=np.float32),
        "iota16": np.tile(np.arange(16, dtype=np.float32)[None, :], (128, 1)),
    }


WNAMES = ["ada_w", "ada_b", "norm_mix", "w_in", "fourier_w", "pool_w", "pool_scale", "pool_proj", "q_norm", "k_norm",
          "attn_proj", "w_out", "norm_ffn", "peer_wq", "peer_keys", "peer_u", "peer_v"]


def make_in_maps(inputs, ncores=8):
    consts = host_consts()
    w = {k: np.ascontiguousarray(np.asarray(inputs[k], dtype=np.float32)) for k in WNAMES}
    x = np.asarray(inputs["x"], dtype=np.float32)
    c = np.asarray(inputs["c"], dtype=np.float32)
    ctx = np.asarray(inputs["ctx"], dtype=np.float32)
    c_ctx = np.asarray(inputs["c_ctx"], dtype=np.float32)
    maps = []
    for core in range(ncores):
        b0 = 2 * core
        cc = np.stack([c[b0], c[b0 + 1], c_ctx], axis=0)
        cct = np.ascontiguousarray(cc.reshape(3, 8, 128).transpose(2, 1, 0).reshape(128, 24))
        m = {"x": np.ascontiguousarray(x[b0:b0 + 2]), "ctx": np.ascontiguousarray(ctx[b0:b0 + 2]), "cct": cct}
        m.update(w)
        m.update(consts)
        maps.append(m)
    return maps


def kernel(**inputs):
    nc, _ = build(depth=DEPTH, WD=DEPTH)
    maps = make_in_maps(inputs, 8)
    res = run_bass_kernel_spmd(nc, maps, core_ids=list(range(8)))
    out = np.concatenate([np.asarray(r["y"]) for r in res.results], axis=0)
    return out.astype(np.float32)
```
